# Optimizing a Trainium2 kernel written in Bass

```python
import math
import jax, jax.numpy as jnp
from jax import lax
import numpy as np

D_MODEL = 4096
BATCH = 4
SEQ = 2048
DEPTH = 2
DEC_BATCH = 8
DEC_SEQ = 1
PAST_LEN = 16384
PAGE_SIZE = 128

HEAD_DIM = 128
N_HEADS = 8
ATT_GROUPS = ((128, 1), (512, 4), (2048, 16))
N_GROUPS = len(ATT_GROUPS)
ATT_W = N_HEADS * HEAD_DIM
POOL_WINDOWS = (2, 4, 8, 16)
POOL_W = D_MODEL // 2
POOL_GC = POOL_W // len(POOL_WINDOWS)
POOL_CTX = max(POOL_WINDOWS) - 1
D_FF = 11008
CONV_W = 3
GATE_OFF = POOL_W + 3 * N_GROUPS * ATT_W
IN_W = GATE_OFF + 2 * D_MODEL
ALPHA = (2.0 * DEPTH) ** 0.25
BETA = (8.0 * DEPTH) ** -0.25
LN_EPS = 1e-5
NEG = -1e30

kernel_name = 'hybrid_pool_dilated_attn_convffn_step'


def _layer_norm(x, g, b):
    xf = x.astype(jnp.float32)
    mu = jnp.mean(xf, axis=-1, keepdims=True)
    var = jnp.mean(jnp.square(xf - mu), axis=-1, keepdims=True)
    y = (xf - mu) * lax.rsqrt(var + LN_EPS) * g.astype(jnp.float32) + b.astype(jnp.float32)
    return y.astype(x.dtype)


def _pool_mix(u_ext, pos0, w_grp, scale):
    B, L, P = u_ext.shape
    T = L - POOL_CTX
    uf = u_ext.astype(jnp.float32)
    c0 = jnp.concatenate([jnp.zeros((B, 1, P), jnp.float32), jnp.cumsum(uf, axis=1)], axis=1)
    n_seen = pos0 + jnp.arange(T, dtype=jnp.float32) + 1.0
    parts = []
    for g, w in enumerate(POOL_WINDOWS):
        c_lo, c_hi = g * POOL_GC, (g + 1) * POOL_GC
        hi = c0[:, POOL_CTX + 1:, c_lo:c_hi]
        lo = c0[:, POOL_CTX + 1 - w:POOL_CTX + 1 - w + T, c_lo:c_hi]
        cnt = jnp.minimum(n_seen, float(w))[None, :, None]
        parts.append((hi - lo) / cnt - uf[:, POOL_CTX:, c_lo:c_hi])
    d = jnp.stack(parts, axis=2)
    y = jnp.einsum('btgc,gce->btge', d, w_grp.astype(jnp.float32)).reshape(B, T, P)
    return (y * scale.astype(jnp.float32)).astype(u_ext.dtype)


def _dilated_band(q, k, v, dil, n):
    B, S, H, Dh = q.shape
    M = S // dil
    C = n
    nb = -(-M // C)
    Mp = nb * C

    def to_res(t):
        t = t.reshape(B, M, dil, H, Dh).transpose(0, 2, 1, 3, 4).reshape(B * dil, M, H, Dh)
        t = jnp.pad(t, ((0, 0), (0, Mp - M), (0, 0), (0, 0)))
        return t.reshape(B * dil, nb, C, H, Dh)

    def with_prev(t):
        prev = jnp.concatenate([jnp.zeros_like(t[:, :1]), t[:, :-1]], axis=1)
        return jnp.concatenate([prev, t], axis=2)

    qb = to_res(q)
    kk = with_prev(to_res(k))
    vv = with_prev(to_res(v))
    s = jnp.einsum('bnqhd,bnkhd->bnhqk', qb, kk).astype(jnp.float32) * (Dh ** -0.5)
    qi = jnp.arange(C)[:, None]
    kj = jnp.arange(2 * C)[None, :]
    dist = C + qi - kj
    blk = jnp.arange(nb)[:, None, None]
    valid = (dist >= 0) & (dist <= n) & ((blk - 1) * C + kj >= 0)
    s = jnp.where(valid[None, :, None], s, NEG)
    lse = jax.nn.logsumexp(s, axis=-1)
    p = jnp.exp(s - lse[..., None])
    o = jnp.einsum('bnhqk,bnkhd->bnqhd', p.astype(vv.dtype), vv)
    o = o.reshape(B, dil, Mp, H, Dh)[:, :, :M].transpose(0, 2, 1, 3, 4).reshape(B, S, H, Dh)
    lse = lse.transpose(0, 1, 3, 2).reshape(B, dil, Mp, H)[:, :, :M]
    lse = lse.transpose(0, 2, 1, 3).reshape(B, S, H)
    return o, lse


def _dilated_gather(q, kv_ext, dil, n):
    B, T, H, Dh = q.shape
    L = kv_ext.shape[1] - T
    idx = L + jnp.arange(T)[:, None] - dil * jnp.arange(n + 1)[None, :]
    valid = idx >= 0
    g = kv_ext[:, jnp.clip(idx, 0, None)]
    kg, vg = g[:, :, :, 0], g[:, :, :, 1]
    s = jnp.einsum('bqhd,bqkhd->bqhk', q, kg).astype(jnp.float32) * (Dh ** -0.5)
    s = jnp.where(valid[None, :, None, :], s, NEG)
    lse = jax.nn.logsumexp(s, axis=-1)
    p = jnp.exp(s - lse[..., None])
    o = jnp.einsum('bqhk,bqkhd->bqhd', p.astype(vg.dtype), vg)
    return o, lse


def _mixer(x, pool_prev, kv_prev, pos0, w_in, w_pool_grp, pool_scale, w_br_pool, w_br_att, w_out):
    B, T, _ = x.shape
    h = jnp.einsum('btd,de->bte', x, w_in)
    u_ext = jnp.concatenate([pool_prev, h[..., :POOL_W]], axis=1)
    y_pool = _pool_mix(u_ext, pos0, w_pool_grp, pool_scale)
    new_pool = u_ext[:, -POOL_CTX:]
    outs, lses, new_kv = [], [], []
    for gi, (win, dil) in enumerate(ATT_GROUPS):
        base = POOL_W + 3 * gi * ATT_W
        q = h[..., base:base + ATT_W].reshape(B, T, N_HEADS, HEAD_DIM)
        k = h[..., base + ATT_W:base + 2 * ATT_W].reshape(B, T, N_HEADS, HEAD_DIM)
        v = h[..., base + 2 * ATT_W:base + 3 * ATT_W].reshape(B, T, N_HEADS, HEAD_DIM)
        kv_new = jnp.stack([k, v], axis=2)
        n = win // dil
        if kv_prev is None:
            o, lse = _dilated_band(q, k, v, dil, n)
            kv_all = kv_new
        else:
            kv_all = jnp.concatenate([kv_prev[gi], kv_new], axis=1)
            o, lse = _dilated_gather(q, kv_all, dil, n)
        new_kv.append(kv_all[:, -min(win, pos0 + T):])
        outs.append(o.astype(jnp.float32))
        lses.append(lse)
    wg = jax.nn.softmax(jnp.stack(lses, axis=0), axis=0)
    o = jnp.sum(wg[..., None] * jnp.stack(outs, axis=0), axis=0)
    o = o.reshape(B, T, ATT_W).astype(x.dtype)
    g_pool = jax.nn.sigmoid(h[..., GATE_OFF:GATE_OFF + D_MODEL].astype(jnp.float32))
    g_att = jax.nn.sigmoid(h[..., GATE_OFF + D_MODEL:].astype(jnp.float32))
    merged = (g_pool * jnp.einsum('btp,pd->btd', y_pool, w_br_pool).astype(jnp.float32)
              + g_att * jnp.einsum('bta,ad->btd', o, w_br_att).astype(jnp.float32))
    out = jnp.einsum('btd,de->bte', merged.astype(x.dtype), w_out)
    return out, new_pool, new_kv


def _conv_ffn(x, conv_prev, w_up, conv_w, conv_b, w_down):
    T = x.shape[1]
    hu = jnp.einsum('btd,df->btf', x, w_up)
    a, b = hu[..., :D_FF], hu[..., D_FF:]
    a_ext = jnp.concatenate([conv_prev, a], axis=1)
    ac = (conv_w[0] * a_ext[:, 0:T] + conv_w[1] * a_ext[:, 1:T + 1]
          + conv_w[2] * a_ext[:, 2:T + 2] + conv_b)
    hm = jax.nn.gelu(ac.astype(jnp.float32), approximate=False) * b.astype(jnp.float32)
    y = jnp.einsum('btf,fd->btd', hm.astype(x.dtype), w_down)
    return y, a_ext[:, -(CONV_W - 1):]


def setup_inputs(seed: int = 0) -> dict:
    key = jax.random.key(seed)
    ks = jax.random.split(key, 24)
    f32 = jnp.float32

    def nrm(k, shape, scale):
        return jax.random.normal(k, shape, f32) * scale

    kvshape = lambda w: (DEPTH, DEC_BATCH, min(w, PAST_LEN), 2, N_HEADS, HEAD_DIM)
    return {
        'x_prompt': nrm(ks[0], (BATCH, SEQ, D_MODEL), 1.0),
        'x_sample': nrm(ks[1], (DEC_BATCH, DEC_SEQ, D_MODEL), 1.0),
        'state_pool': nrm(ks[2], (DEPTH, DEC_BATCH, POOL_CTX, POOL_W), 1.0),
        'cache_kv1': nrm(ks[3], kvshape(ATT_GROUPS[0][0]), 1.0),
        'cache_kv2': nrm(ks[4], kvshape(ATT_GROUPS[1][0]), 1.0),
        'cache_kv3': nrm(ks[5], kvshape(ATT_GROUPS[2][0]), 1.0),
        'state_conv': nrm(ks[6], (DEPTH, DEC_BATCH, CONV_W - 1, D_FF), 1.0),
        'w_in': nrm(ks[7], (DEPTH, D_MODEL, IN_W), D_MODEL ** -0.5),
        'w_pool_grp': nrm(ks[8], (DEPTH, len(POOL_WINDOWS), POOL_GC, POOL_GC), POOL_GC ** -0.5),
        'pool_scale': 1.0 + nrm(ks[9], (DEPTH, POOL_W), 0.1),
        'w_br_pool': nrm(ks[10], (DEPTH, POOL_W, D_MODEL), POOL_W ** -0.5),
        'w_br_att': nrm(ks[11], (DEPTH, ATT_W, D_MODEL), ATT_W ** -0.5),
        'w_out': nrm(ks[12], (DEPTH, D_MODEL, D_MODEL), BETA * D_MODEL ** -0.5),
        'ln1_g': 1.0 + nrm(ks[13], (DEPTH, D_MODEL), 0.05),
        'ln1_b': nrm(ks[14], (DEPTH, D_MODEL), 0.02),
        'w_up': nrm(ks[15], (DEPTH, D_MODEL, 2 * D_FF), D_MODEL ** -0.5),
        'conv_w': nrm(ks[16], (DEPTH, CONV_W, D_FF), CONV_W ** -0.5),
        'conv_b': nrm(ks[17], (DEPTH, D_FF), 0.02),
        'w_down': nrm(ks[18], (DEPTH, D_FF, D_MODEL), BETA * D_FF ** -0.5),
        'ln2_g': 1.0 + nrm(ks[19], (DEPTH, D_MODEL), 0.05),
        'ln2_b': nrm(ks[20], (DEPTH, D_MODEL), 0.02),
    }


def reference(x_prompt, x_sample, state_pool, cache_kv1, cache_kv2, cache_kv3, state_conv,
              w_in, w_pool_grp, pool_scale, w_br_pool, w_br_att, w_out, ln1_g, ln1_b,
              w_up, conv_w, conv_b, w_down, ln2_g, ln2_b):
    caches = (cache_kv1, cache_kv2, cache_kv3)

    def run(x, pos0, sample):
        Bn = x.shape[0]
        pools, convs = [], []
        kvs = [[] for _ in ATT_GROUPS]
        for l in range(DEPTH):
            if sample:
                pool_prev = state_pool[l].astype(x.dtype)
                kv_prev = [c[l].astype(x.dtype) for c in caches]
                conv_prev = state_conv[l].astype(x.dtype)
            else:
                pool_prev = jnp.zeros((Bn, POOL_CTX, POOL_W), x.dtype)
                kv_prev = None
                conv_prev = jnp.zeros((Bn, CONV_W - 1, D_FF), x.dtype)
            m, new_pool, new_kv = _mixer(x, pool_prev, kv_prev, pos0, w_in[l], w_pool_grp[l],
                                         pool_scale[l], w_br_pool[l], w_br_att[l], w_out[l])
            x = _layer_norm(ALPHA * x + m, ln1_g[l], ln1_b[l])
            f, new_conv = _conv_ffn(x, conv_prev, w_up[l], conv_w[l], conv_b[l], w_down[l])
            x = _layer_norm(ALPHA * x + f, ln2_g[l], ln2_b[l])
            pools.append(new_pool)
            convs.append(new_conv)
            for gi in range(N_GROUPS):
                kvs[gi].append(new_kv[gi])
        return (x, jnp.stack(pools), jnp.stack(kvs[0]), jnp.stack(kvs[1]),
                jnp.stack(kvs[2]), jnp.stack(convs))

    y_prompt, pool_p, kv1_p, kv2_p, kv3_p, conv_p = run(x_prompt, 0, False)
    y_sample, pool_s, kv1_s, kv2_s, kv3_s, conv_s = run(x_sample, PAST_LEN, True)
    return (y_prompt, y_sample, pool_p, pool_s, kv1_p, kv1_s, kv2_p, kv2_s, kv3_p, kv3_s, conv_p, conv_s)
```

```python
import numpy as np
from contextlib import ExitStack
import concourse.bass as bass
import concourse.mybir as mybir
from concourse.bass_utils import run_bass_kernel_spmd
import ml_dtypes

F32 = mybir.dt.float32
BF16 = mybir.dt.bfloat16
AF = mybir.ActivationFunctionType
ALU = mybir.AluOpType
AX = mybir.AxisListType

D = 4096; KC = 32; T = 512; NT = 4; SEQ = 2048; DFF = 11008; FC = 86; H = 8; DH = 128
L = 2; GATE_OFF = 2048 + 9216
CS = 514
ALPHA = (2.0 * L) ** 0.25; EPS = 1e-5; SCALE = DH ** -0.5
DILS = (1, 4, 16); CLEN = (128, 512, 2048); PWIN = (2, 4, 8, 16)
SLOT = 8192; NSLOT = 3
DKC = [16, 16, 16, 16, 16, 6]


def wplan():
    p = []
    p += [SLOT] * 32
    p += [SLOT] * 12
    p += [2048] * 4
    for _ in range(32):
        p += [8192, 3072]
    p += [SLOT] * 16
    p += [SLOT] * FC
    for _ in range(8):
        p += [k * 512 for k in DKC]
    return p


def _blk(W, r0, nk, c0, nc_):
    a = W[r0:r0 + nk * 128, c0:c0 + nc_]
    return np.ascontiguousarray(a.reshape(nk, 128, nc_).transpose(1, 0, 2)).reshape(128, nk * nc_)


def a_chunk_cols():
    cols = [c * 128 for c in range(16)]
    for g in range(3):
        base = 2048 + 3 * g * 1024
        cols += [base + h * 128 for h in range(8)]
        cols += [base + 1024 + h * 128 for h in range(8)]
    return cols


def build_wstream(l, w_in, w_pool_grp, w_br_pool, w_br_att, w_out, w_up, w_down):
    out = []
    cols = a_chunk_cols()
    for i in range(32):
        assert cols[2 * i + 1] == cols[2 * i] + 128
        out.append(_blk(w_in[l], 0, 32, cols[2 * i], 256))
    for g in range(3):
        for half in range(2):
            c0 = 2048 + 3 * g * 1024 + 2048 + half * 512
            for kb in range(2):
                out.append(_blk(w_in[l], kb * 2048, 16, c0, 512))
    for g in range(4):
        out.append(_blk(w_pool_grp[l, g], 0, 4, 0, 512))
    for m in range(32):
        out.append(np.concatenate([_blk(w_in[l], 0, 32, GATE_OFF + m * 128, 128),
                                   _blk(w_in[l], 0, 32, GATE_OFF + D + m * 128, 128)], axis=1))
        out.append(np.concatenate([_blk(w_br_pool[l], 0, 16, m * 128, 128),
                                   _blk(w_br_att[l], 0, 8, m * 128, 128)], axis=1))
    for cg in range(8):
        for kb in range(2):
            out.append(_blk(w_out[l], kb * 2048, 16, cg * 512, 512))
    for mf in range(FC):
        out.append(np.concatenate([_blk(w_up[l], 0, 32, mf * 128, 128),
                                   _blk(w_up[l], 0, 32, DFF + mf * 128, 128)], axis=1))
    for cg in range(8):
        k0 = 0
        for nk in DKC:
            out.append(_blk(w_down[l], k0 * 128, nk, cg * 512, 512))
            k0 += nk
    sizes = [o.shape[1] for o in out]
    assert sizes == wplan(), "plan mismatch"
    return np.concatenate(out, axis=1)


def make_consts():
    k = np.arange(128)[:, None]
    q = np.arange(128)[None, :]
    cur = np.tile((k <= q).astype(np.float32), (1, 4))
    prev = np.tile((k >= q).astype(np.float32), (1, 4))
    same = ((k % 4) == (q % 4)).astype(np.float32)
    m3cur = np.tile(same * (k <= q), (1, 4))
    m3hist = np.tile(same, (1, 4))
    masks = [cur, prev, m3cur, m3hist]
    masks = np.stack(masks, axis=1).astype(ml_dtypes.bfloat16)
    invc = np.zeros((128, 4, 15), np.float32)
    for wi, w in enumerate(PWIN):
        for t in range(15):
            invc[:, wi, t] = 1.0 / min(t + 1, w)
    ident = np.eye(128, dtype=np.float32)
    return masks, invc, ident


class Op:
    __slots__ = ("eng", "fn", "deps", "chan", "sig", "need", "stream")


class Sched:
    def __init__(self):
        self.ops = []
        self.acc = {}
        self.chan_last = {}

    def add(self, eng, fn, R=(), W=(), chan=None):
        idx = len(self.ops)
        op = Op()
        op.eng = eng; op.fn = fn; op.chan = chan; op.sig = None; op.need = chan is not None
        op.stream = chan if chan is not None else eng
        deps = set()
        for (sp, lo, hi) in R:
            for e in self.acc.get(sp, ()):
                if (e[3] or (sp == "ps" and e[4] != op.stream)) and e[0] < hi and lo < e[1]:
                    deps.add(e[2])
        for (sp, lo, hi) in W:
            for e in self.acc.get(sp, ()):
                if e[0] < hi and lo < e[1]:
                    deps.add(e[2])
        if chan is not None and chan in self.chan_last:
            deps.add(self.chan_last[chan])
        if chan is not None:
            self.chan_last[chan] = idx
        for (sp, lo, hi) in W:
            Lst = self.acc.setdefault(sp, [])
            Lst[:] = [e for e in Lst if not (lo <= e[0] and e[1] <= hi)]
            Lst.append([lo, hi, idx, True, op.stream])
        for (sp, lo, hi) in R:
            Lst = self.acc.setdefault(sp, [])
            Lst[:] = [e for e in Lst if not ((not e[3]) and e[4] == op.stream and lo <= e[0] and e[1] <= hi)]
            Lst.append([lo, hi, idx, False, op.stream])
        best = {}
        for d in deps:
            o = self.ops[d]
            if o.stream == "pe" and eng == "pe" and chan is None:
                continue
            if o.stream not in best or best[o.stream] < d:
                best[o.stream] = d
        op.deps = sorted(best.values())
        for d in op.deps:
            self.ops[d].need = True
        self.ops.append(op)
        return idx


class StopBuild(Exception):
    pass


def build_nc(NL=L, NTR=NT, stop=None, ws_cols=None, skip=()):
    nc = bass.Bass("TRN2", target_bir_lowering=False)
    plan = wplan()
    WTOT = sum(plan)
    woffs = np.concatenate([[0], np.cumsum(plan)]).tolist()

    def din(name, shape, dt=F32):
        return nc.dram_tensor(name, list(shape), dt, kind="ExternalInput").ap()

    def dout(name, shape, dt=F32):
        return nc.dram_tensor(name, list(shape), dt, kind="ExternalOutput").ap()

    ws = din("ws", [NL, 128, WTOT if ws_cols is None else ws_cols])
    xT0 = din("xT0", [128, KC, SEQ])
    xrow = din("xrow", [SEQ, D])
    xsT = din("xsT", [128, KC])
    spool = din("spool", [L, 128, 16, 15])
    sconv = din("sconv", [L, 128, 2, FC])
    ck = [din(f"ck{g}", [L, CLEN[g], 2048]) for g in range(3)]
    pscale = din("pscale", [L, 128, 16])
    convw = din("convw", [L, 128, 3, FC])
    convb = din("convb", [L, 128, FC])
    lnrow = din("lnrow", [L, 4, D])
    lnfm = din("lnfm", [L, 128, 4, KC])
    masks_d = din("masks", [128, 4, 512], BF16)
    invc_d = din("invc", [128, 4, 15])
    ident_d = din("ident", [128, 128])

    y = dout("y", [SEQ, D])
    ysT = dout("ysT", [128, KC])
    poolp = dout("poolp", [L, 128, 16, 15])
    pools = dout("pools", [L, 128, 16, 15])
    kT_o = dout("kT", [L, 3, 128, H, SEQ])
    v_o = dout("vo", [L, 3, 16, 128, 1024])
    kvs = [dout(f"kvs{g}", [L, CLEN[g], 2048]) for g in range(3)]
    convp = dout("convp", [L, 128, FC, 2])
    convs = dout("convs", [L, 128, 2, FC])

    xpre = nc.dram_tensor("xpre", [SEQ, D], F32).ap()
    xmid = nc.dram_tensor("xmid", [SEQ, D], F32).ap()
    x1row = nc.dram_tensor("x1row", [SEQ, D], F32).ap()
    xT1 = nc.dram_tensor("xT1", [128, KC, SEQ], BF16).ap()
    KTs = nc.dram_tensor("KTs", [L, 3, 128, H, SEQ], BF16).ap()
    Vs = nc.dram_tensor("Vs", [L, 3, 16, 128, 1024], BF16).ap()

    es = ExitStack()
    SBTOT = 212800
    arena = es.enter_context(nc.sbuf_tensor("arena", [128, SBTOT // 2], BF16))
    psum = es.enter_context(nc.psum_tensor("psum", [128, 4096], F32))

    def sbv(off, dt, *dims):
        n = int(np.prod(dims))
        esz = 4 if dt == F32 else 2
        assert off % 4 == 0
        a = arena[:, off // 2: off // 2 + n * esz // 2]
        if dt == F32:
            a = a.bitcast(F32)
        if len(dims) == 2:
            a = a.rearrange("p (a b) -> p a b", b=dims[1])
        elif len(dims) == 3:
            a = a.rearrange("p (a b c) -> p a b c", b=dims[1], c=dims[2])
        return a

    def rs(off, nbytes):
        return ("sb", off, off + nbytes)

    def rp(bank, lo=0, hi=512):
        return ("ps", bank * 2048, bank * 2048 + 2048)

    def pb(bank, lo=0, hi=512):
        return psum[:, bank * 512 + lo: bank * 512 + hi]

    o_ident = 0; o_ones = 512; o_masks = 1024; o_invc = 7168
    o_pscale = 7680; o_convw = 7744; o_convb = 8776; o_lnfm = 9120
    o_uhist = 10240; o_ahist = 11264; o_stats = 12032; o_small = 12800; o_samp = 14848
    o_W = 19456
    o_A = o_W + NSLOT * SLOT * 2
    o_B = o_A + (KC * CS * 2)
    o_C = o_B + 88448
    assert o_C + 22720 <= SBTOT

    ident = sbv(o_ident, F32, 128)
    ones_bf = sbv(o_ones, BF16, 128)
    masks = sbv(o_masks, BF16, 4, 512)
    invc = sbv(o_invc, F32, 4, 15)
    pscale_t = sbv(o_pscale, F32, 16)
    convw_t = sbv(o_convw, F32, 3, FC)
    convb_t = sbv(o_convb, F32, FC)
    lnfm_t = sbv(o_lnfm, F32, 4, KC)
    uhist = sbv(o_uhist, F32, 16, 15)
    ahist = sbv(o_ahist, F32, FC, 2)
    stats = sbv(o_stats, F32, 4, 8, 6)
    XT = sbv(o_A, BF16, KC, CS)
    o_dT = o_B; o_ypT = o_B + (16 * CS * 2); o_oT = o_B + (KC * CS * 2); o_QT = o_B + 41120
    o_KTh = o_B + 65696; o_Vh = o_B
    o_mg = o_B + 41120
    o_sc = o_B + 74016
    dT = sbv(o_dT, BF16, 16, CS)
    ypT = sbv(o_ypT, BF16, 16, CS)
    oT = sbv(o_oT, BF16, 8, CS)
    QT = sbv(o_QT, BF16, 3, 8, 512)
    mgT = sbv(o_mg, BF16, KC, CS)
    hmT = sbv(o_B, BF16, FC, CS)
    o_xsres = o_samp; o_hs = o_samp + 128; o_s2 = o_samp + 128 + 704
    xsres = sbv(o_xsres, F32, KC)
    hs = sbv(o_hs, F32, 176)
    s2 = sbv(o_s2, F32, 800)

    S = Sched()
    bank_ctr = [0]
    BANKS = [0, 1, 2, 3, 5, 6, 7]

    def nbank():
        b = BANKS[bank_ctr[0] % len(BANKS)]
        bank_ctr[0] += 1
        return b
    ALLB = [0, 1, 2, 3, 5, 6, 7]
    SB = 4

    wstate = {"issued": 0, "next": 0}
    TOTBLK = len(plan) * NL * NTR

    def w_issue(i):
        l = i // (len(plan) * NTR)
        j = i % len(plan)
        n = plan[j]
        slot = i % NSLOT
        dst = sbv(o_W + slot * SLOT * 2, BF16, n)
        src = ws[l, :, woffs[j]: woffs[j] + n]
        S.add("pool", lambda e, dst=dst, src=src: e.dma_start(out=dst, in_=src),
              W=[rs(o_W + slot * SLOT * 2, n * 2)], chan=f"w{slot}")

    def w_get(n):
        i = wstate["next"]
        assert plan[i % len(plan)] == n, (i, plan[i % len(plan)], n)
        while wstate["issued"] < min(TOTBLK, i + NSLOT - 1):
            w_issue(wstate["issued"])
            wstate["issued"] += 1
        wstate["next"] += 1
        slot = i % NSLOT
        off = o_W + slot * SLOT * 2
        return off, sbv(off, BF16, n)

    def dma(eng, chan, out, in_, R=(), W=(), nonc=False):
        if nonc:
            S.add(eng, lambda e: e.dma_start(out=out, in_=in_, allow_slow_non_contiguous=True), R=R, W=W, chan=chan)
        else:
            S.add(eng, lambda e: e.dma_start(out=out, in_=in_), R=R, W=W, chan=chan)

    import sys as _sys
    DBGMAP = build_nc.dbgmap = {}

    def mmgroup(items, R, W):
        lab = (_sys._getframe(1).f_lineno, _sys._getframe(2).f_lineno)

        def fn(e):
            last = None
            n0 = nc.get_next_instruction_name()
            for ii, (o, a, b, st, sp) in enumerate(items):
                last = e.matmul(o, lhsT=a, rhs=b, start=st, stop=sp, skip_group_check=True)
            DBGMAP[(n0, nc.get_next_instruction_name())] = (lab, len(items))
            return last
        S.add("pe", fn, R=R, W=W)

    evq = [0]

    def ev_eng():
        evq[0] += 1
        return "act" if evq[0] % 2 else "dve"

    def copy_op(eng, out, in_, R, W):
        if eng == "act":
            S.add("act", lambda e: e.activation(out=out, in_=in_, func=AF.Copy), R=R, W=W)
        else:
            S.add(eng, lambda e: e.tensor_copy(out=out, in_=in_), R=R, W=W)

    def tt(eng, out, in0, in1, op, R, W):
        S.add(eng, lambda e: e.tensor_tensor(out=out, in0=in0, in1=in1, op=op), R=R, W=W)

    def ts(eng, out, in0, s1, s2_, op0, op1, R, W):
        if s2_ is None:
            S.add(eng, lambda e: e.tensor_scalar(out=out, in0=in0, scalar1=s1, scalar2=None, op0=op0), R=R, W=W)
        else:
            S.add(eng, lambda e: e.tensor_scalar(out=out, in0=in0, scalar1=s1, scalar2=s2_, op0=op0, op1=op1), R=R, W=W)

    def stt(out, in0, sc, in1, op0, op1, R, W):
        S.add("dve", lambda e: e.scalar_tensor_tensor(out=out, in0=in0, scalar=sc, in1=in1, op0=op0, op1=op1), R=R, W=W)

    def act(out, in_, func, R, W, bias=None, scale=None):
        kw = {}
        if bias is not None:
            kw["bias"] = bias
        if scale is not None:
            kw["scale"] = scale
        S.add("act", lambda e: e.activation(out=out, in_=in_, func=func, **kw), R=R, W=W)

    for c0_ in range(0, SBTOT // 2, 16384):
        c1_ = min(SBTOT // 2, c0_ + 16384)
        S.add("dve", lambda e, c0_=c0_, c1_=c1_: e.memset(arena[:, c0_:c1_], 0.0), W=[rs(c0_ * 2, c1_ * 2 - c0_ * 2)])
    dma("sp", "c0", ident, ident_d, W=[rs(o_ident, 512)])
    dma("sp", "c1", masks, masks_d, W=[rs(o_masks, 4096)])
    dma("sp", "c2", invc, invc_d, W=[rs(o_invc, 240)])
    S.add("dve", lambda e: e.memset(ones_bf, 1.0), W=[rs(o_ones, 256)])
    R_ident = rs(o_ident, 512); R_ones = rs(o_ones, 256)

    sample_first = [True]

    def sample_mm_items(col, lhs_list, rhs_list):
        items = []
        n = len(lhs_list)
        for i in range(n):
            st = sample_first[0]
            sample_first[0] = False
            items.append((pb(SB, col, col + 1), lhs_list[i], rhs_list[i], st, i == n - 1))
        return items

    def ckpt(name, l, t):
        if stop is not None and stop == (name, l, t):
            raise StopBuild()

    try:
      for l in range(NL):
        dma("sp", "c0", pscale_t, pscale[l], W=[rs(o_pscale, 64)])
        dma("sp", "c1", convw_t, convw[l], W=[rs(o_convw, 1032)])
        dma("sp", "c2", convb_t, convb[l], W=[rs(o_convb, 344)])
        dma("sp", "c0", lnfm_t, lnfm[l], W=[rs(o_lnfm, 512)])
        S.add("dve", lambda e: e.memset(uhist, 0.0), W=[rs(o_uhist, 960)])
        S.add("dve", lambda e: e.memset(ahist, 0.0), W=[rs(o_ahist, 688)])

        for t in range(NTR):
            p0 = t * T
            t0 = (t == 0)
            NCOL = 513 if t0 else 512
            R_XT = rs(o_A, (KC * CS * 2))

            if l == 0:
                dma("pool", "x0", XT[:, :, 0:512], xT0[:, :, p0:p0 + T], W=[R_XT])
            else:
                dma("sp", "x0", XT[:, :, 0:512], xT1[:, :, p0:p0 + T],
                    R=[("xT1", p0, p0 + T)], W=[R_XT])
            if t0:
                if l == 0:
                    dma("sp", "c1", xsres, xsT, W=[rs(o_xsres, 128)])
                copy_op("dve", XT[:, :, 512], xsres, R=[rs(o_xsres, 128)], W=[R_XT])
                sample_first[0] = True

            ckpt("P0", l, t)
            def f_chunk(blk_off, nbytes, lhs_fn, nk, rhs_fn, R_act, scol):
                b = nbank()
                items = [(pb(b), lhs_fn(kc), rhs_fn(kc, 0, 512), kc == 0, kc == nk - 1) for kc in range(nk)]
                W_ = [rp(b)]
                if t0 and "sample" not in skip:
                    items += sample_mm_items(scol, [lhs_fn(kc) for kc in range(nk)],
                                             [rhs_fn(kc, 512, 513) for kc in range(nk)])
                    W_.append(rp(SB, scol, scol + 1))
                mmgroup(items, R=[rs(blk_off, nbytes)] + R_act, W=W_)
                return b

            ue = sbv(o_C + 6144, F32, 527); tA = sbv(o_C + 8256, F32, 527); tB = sbv(o_C + 10368, F32, 527)
            R_ue = rs(o_C + 6144, 2108); R_tA = rs(o_C + 8256, 2108); R_tB = rs(o_C + 10368, 2108)
            kst = [sbv(o_C + i * 2048, F32, 512) for i in range(2)]
            kbs = [sbv(o_C + 4096 + i * 1024, BF16, 512) for i in range(2)]
            kcount = 0
            for bi in range(32):
                boff, bv = w_get(SLOT)
                b3 = bv.rearrange("p (k c) -> p k c", c=256)
                for sub in range(2):
                    ci = bi * 2 + sub
                    bank = f_chunk(boff, SLOT * 2, lambda kc, b3=b3, sub=sub: b3[:, kc, sub * 128:(sub + 1) * 128], KC,
                                   lambda kc, lo, hi: XT[:, kc, lo:hi], [R_XT], ci)
                    if ci < 16 and "pool" in skip:
                        pass
                    elif ci >= 16 and "qk" in skip:
                        pass
                    elif ci < 16:
                        c = ci
                        wi = c // 4
                        w = PWIN[wi]
                        R_uh = rs(o_uhist + c * 60, 60)
                        copy_op("dve", ue[:, 0:15], uhist[:, c, :], R=[R_uh], W=[R_ue])
                        copy_op("act", ue[:, 15:527], pb(bank), R=[rp(bank)], W=[R_ue])
                        src, Rsrc = ue, R_ue
                        bufs = [(tA, R_tA), (tB, R_tB)]
                        sh = 1
                        k = 0
                        while sh < w:
                            dst, Rdst = bufs[k % 2]
                            tt("dve", dst[:, sh:527], src[:, sh:527], src[:, 0:527 - sh], ALU.add, R=[Rsrc], W=[Rdst])
                            src, Rsrc = dst, Rdst
                            sh *= 2
                            k += 1
                        R_d = rs(o_dT + c * (CS * 2), (CS * 2))
                        stt(dT[:, c, 0:512], src[:, 15:527], 1.0 / w, ue[:, 15:527], ALU.mult, ALU.subtract,
                            R=[Rsrc, R_ue], W=[R_d])
                        if t0:
                            dst, Rdst = bufs[k % 2]
                            tt("dve", dst[:, 0:15], src[:, 15:30], invc[:, wi, :], ALU.mult, R=[Rsrc, rs(o_invc, 240)], W=[Rdst])
                            tt("dve", dT[:, c, 0:15], dst[:, 0:15], ue[:, 15:30], ALU.subtract, R=[Rdst, R_ue], W=[R_d])
                        copy_op("dve", uhist[:, c, :], ue[:, 512:527], R=[R_ue], W=[R_uh])
                    else:
                        j = ci - 16
                        g = j // 16
                        isk = (j % 16) >= 8
                        h = j % 8
                        if not isk:
                            copy_op(ev_eng(), QT[:, g, h, :], pb(bank), R=[rp(bank)],
                                    W=[rs(o_QT + (g * 8 + h) * 1024, 1024)])
                        else:
                            s_ = kcount % 2
                            kcount += 1
                            copy_op("act", kst[s_], pb(bank), R=[rp(bank)], W=[rs(o_C + s_ * 2048, 2048)])
                            copy_op("dve", kbs[s_], kst[s_], R=[rs(o_C + s_ * 2048, 2048)], W=[rs(o_C + 4096 + s_ * 1024, 1024)])
                            dma("sp", f"kst{s_}", kT_o[l, g, :, h, p0:p0 + T], kst[s_], R=[rs(o_C + s_ * 2048, 2048)])
                            if "kts" not in skip:
                              dma("sp", f"kbs{s_}", KTs[l, g, :, h, p0:p0 + T], kbs[s_],
                                  R=[rs(o_C + 4096 + s_ * 1024, 1024)], W=[(f"KTs{l}{g}{h}", p0, p0 + T)])

            ckpt("A", l, t)
            vst = [sbv(o_C + 12480 + i * 2048, F32, 512) for i in range(2)]
            vbs = [sbv(o_C + 16576 + i * 1024, BF16, 512) for i in range(2)]
            vcount = 0
            for g in range(3):
                for half in range(2):
                    blks = [w_get(SLOT), w_get(SLOT)]
                    banks = [nbank() for _ in range(4)]
                    for kb in range(2):
                        boff, bv = blks[kb]
                        b3 = bv.rearrange("p (k c) -> p k c", c=512)
                        items = []
                        for kk in range(16):
                            kc = kb * 16 + kk
                            for tb in range(4):
                                if g == 0:
                                    lhs = XT[:, kc, tb * 128:(tb + 1) * 128]
                                else:
                                    lhs = XT[:, kc, 0:512].rearrange("p (i r) -> p r i", r=4)[:, tb, :]
                                items.append((pb(banks[tb]), lhs, b3[:, kk, :], kc == 0, kc == KC - 1))
                        W_ = [rp(bk) for bk in banks]
                        if t0:
                            for sub in range(4):
                                hcol = 64 + g * 8 + half * 4 + sub
                                items += sample_mm_items(hcol, [b3[:, kk, sub * 128:(sub + 1) * 128] for kk in range(16)],
                                                         [XT[:, kb * 16 + kk, 512:513] for kk in range(16)])
                                W_.append(rp(SB, hcol, hcol + 1))
                        mmgroup(items, R=[rs(boff, SLOT * 2), R_XT], W=W_)
                    for tb in range(4):
                        s_ = vcount % 2
                        vcount += 1
                        Rv = rs(o_C + 12480 + s_ * 2048, 2048); Rb = rs(o_C + 16576 + s_ * 1024, 1024)
                        copy_op("act", vst[s_], pb(banks[tb]), R=[rp(banks[tb])], W=[Rv])
                        copy_op("dve", vbs[s_], vst[s_], R=[Rv], W=[Rb])
                        c0 = half * 512
                        blk_i = t * 4 + tb
                        dma("sp", f"vst{s_}", v_o[l, g, blk_i, :, c0:c0 + 512], vst[s_], R=[Rv])
                        dma("sp", f"vbs{s_}", Vs[l, g, blk_i, :, c0:c0 + 512], vbs[s_], R=[Rb],
                            W=[(f"Vs{l}{g}", blk_i * 128, blk_i * 128 + 128)])

            ckpt("V", l, t)
            if t0:
                copy_op("dve", hs[:, 0:88], pb(SB, 0, 88), R=[rp(SB, 0, 88)], W=[rs(o_hs, 352)])
                R_hs = rs(o_hs, 352)
                for g in range(3):
                    Lc = CLEN[g]
                    dma("sp", f"cc{g}", kvs[g][l, 0:Lc - 1, :], ck[g][l, 1:Lc, :])
                    dma("sp", f"cn{g}", kvs[g][l, Lc - 1, 0:1024].rearrange("(h d) -> d h", d=128),
                        hs[:, 16 + 16 * g + 8:16 + 16 * g + 16], R=[R_hs], nonc=True)
                    dma("sp", f"cn{g}", kvs[g][l, Lc - 1, 1024:2048].rearrange("(h d) -> d h", d=128),
                        hs[:, 64 + 8 * g:64 + 8 * g + 8], R=[R_hs], nonc=True)
                sp_h = sbv(o_sc, F32, 16, 15)
                R_sph = rs(o_sc, 960)
                dma("sp", "c2", sp_h, spool[l], W=[R_sph])
                red = s2[:, 0:16]; R_s2 = rs(o_s2, 3200)
                for wi, w in enumerate(PWIN):
                    S.add("dve", lambda e, wi=wi, w=w: e.tensor_reduce(
                        out=red[:, 4 * wi:4 * wi + 4], in_=sp_h[:, 4 * wi:4 * wi + 4, 15 - (w - 1):15], axis=AX.X, op=ALU.add),
                        R=[R_sph], W=[R_s2])
                    tt("dve", red[:, 4 * wi:4 * wi + 4], red[:, 4 * wi:4 * wi + 4], hs[:, 4 * wi:4 * wi + 4], ALU.add,
                       R=[R_s2, R_hs], W=[R_s2])
                    stt(dT[:, 4 * wi:4 * wi + 4, 512], red[:, 4 * wi:4 * wi + 4], 1.0 / w, hs[:, 4 * wi:4 * wi + 4],
                        ALU.mult, ALU.subtract, R=[R_s2, R_hs], W=[rs(o_dT, (16 * CS * 2))])
                po = s2[:, 16:16 + 240].rearrange("p (c r) -> p c r", r=15)
                copy_op("dve", po[:, :, 0:14], sp_h[:, :, 1:15], R=[R_sph], W=[R_s2])
                copy_op("dve", po[:, :, 14], hs[:, 0:16], R=[R_hs], W=[R_s2])
                dma("sp", "c0", pools[l], po, R=[R_s2])

                ckf = sbv(o_sc + 1024, F32, 2048)
                R_ckf = rs(o_sc + 1024, 8192)
                vb = sbv(o_sc + 1024 + 8192, BF16, 1024); R_vb = rs(o_sc + 9216, 2048)
                kts = sbv(o_sc + 11264, BF16, 8, 128); R_kts = rs(o_sc + 11264, 2048)
                qb_ = s2[:, 300:324]
                qbf = sbv(o_samp + 3840, BF16, 24); R_qbf = rs(o_samp + 3840, 48)
                for g in range(3):
                    copy_op("dve", qbf[:, 8 * g:8 * g + 8], hs[:, 16 + 16 * g:16 + 16 * g + 8], R=[R_hs], W=[R_qbf])
                prod = sbv(o_samp + 3888, BF16, 24); R_prod = rs(o_samp + 3888, 48)
                for g in range(3):
                    tt("dve", prod[:, 8 * g:8 * g + 8], hs[:, 16 + 16 * g:16 + 16 * g + 8],
                       hs[:, 16 + 16 * g + 8:16 + 16 * g + 16], ALU.mult, R=[R_hs], W=[R_prod])
                BANKS[:] = [0, 1, 2, 3]
                bS, bU, bZ = 5, 6, 7
                mmgroup([(pb(bZ, 100, 124), ones_bf, prod, True, True)], R=[R_ones, R_prod], W=[rp(bZ, 100, 124)])
                p0e = s2[:, 330:354]
                act(p0e, pb(bZ, 100, 124), AF.Exp, R=[rp(bZ, 100, 124)], W=[R_s2], scale=SCALE)
                first_u = True
                for g in range(3):
                    dil = DILS[g]
                    src = ck[g][l].rearrange("(i r) f -> r i f", r=dil)[0]
                    dma("sp", "ckf", ckf, src, W=[R_ckf])
                    copy_op("act", vb, ckf[:, 1024:2048], R=[R_ckf], W=[R_vb])
                    for h in range(8):
                        bt = nbank()
                        S.add("pe", lambda e, bt=bt, h=h: e.transpose(out=pb(bt, 0, 128), in_=ckf[:, h * 128:(h + 1) * 128], identity=ident),
                              R=[R_ckf, R_ident], W=[rp(bt, 0, 128)])
                        copy_op(ev_eng(), kts[:, h, :], pb(bt, 0, 128), R=[rp(bt, 0, 128)], W=[R_kts])
                    col = g * 8
                    mmgroup([(pb(bS, col + h, col + h + 1), kts[:, h, :], qbf[:, col + h:col + h + 1], (h == 0), True) for h in range(8)],
                            R=[R_kts, R_qbf], W=[rp(bS, col, col + 8)])
                    pg = sbv(o_samp + 3936, BF16, 8); R_pg = rs(o_samp + 3936, 16)
                    act(pg, pb(bS, col, col + 8), AF.Exp, R=[rp(bS, col, col + 8)], W=[R_pg], scale=SCALE)
                    items = []
                    for h in range(8):
                        items.append((pb(bU, col + h, col + h + 1), vb[:, h * 128:(h + 1) * 128], pg[:, h:h + 1], first_u, True))
                        first_u = False
                    items.append((pb(bZ, col, col + 8), ones_bf, pg, True if g == 0 else False, True))
                    mmgroup(items, R=[R_vb, R_pg, R_ones], W=[rp(bU, col, col + 8), rp(bZ, col, col + 8)])
                Us = s2[:, 360:384]; Zs = s2[:, 390:414]
                tt("dve", Us, p0e, hs[:, 64:88], ALU.mult, R=[R_s2, R_hs], W=[R_s2])
                tt("dve", Us, Us, pb(bU, 0, 24), ALU.add, R=[R_s2, rp(bU, 0, 24)], W=[R_s2])
                tt("dve", Zs, p0e, pb(bZ, 0, 24), ALU.add, R=[R_s2, rp(bZ, 0, 24)], W=[R_s2])
                tt("dve", Us[:, 0:8], Us[:, 0:8], Us[:, 8:16], ALU.add, R=[R_s2], W=[R_s2])
                tt("dve", Us[:, 0:8], Us[:, 0:8], Us[:, 16:24], ALU.add, R=[R_s2], W=[R_s2])
                tt("dve", Zs[:, 0:8], Zs[:, 0:8], Zs[:, 8:16], ALU.add, R=[R_s2], W=[R_s2])
                tt("dve", Zs[:, 0:8], Zs[:, 0:8], Zs[:, 16:24], ALU.add, R=[R_s2], W=[R_s2])
                S.add("dve", lambda e: e.reciprocal(out=Zs[:, 0:8], in_=Zs[:, 0:8]), R=[R_s2], W=[R_s2])
                tt("dve", oT[:, :, 512], Us[:, 0:8], Zs[:, 0:8], ALU.mult, R=[R_s2], W=[rs(o_oT, (8 * CS * 2))])
                BANKS[:] = ALLB
                sample_first[0] = True

            ckpt("S", l, t)
            for g4 in range(4):
                boff, bv = w_get(2048)
                b3 = bv.rearrange("p (k c) -> p k c", c=512)
                for sub in range(4):
                    c = g4 * 4 + sub
                    bank = f_chunk(boff, 4096, lambda kc, b3=b3, sub=sub: b3[:, kc, sub * 128:(sub + 1) * 128], 4,
                                   lambda kc, lo, hi, g4=g4: dT[:, 4 * g4 + kc, lo:hi], [rs(o_dT, (16 * CS * 2))], c)
                    act(ypT[:, c, 0:512], pb(bank), AF.Identity, R=[rp(bank), rs(o_pscale, 64)],
                        W=[rs(o_ypT + c * (CS * 2), (CS * 2))], scale=pscale_t[:, c:c + 1])
            if t0:
                tt("dve", ypT[:, :, 512], pb(SB, 0, 16), pscale_t, ALU.mult, R=[rp(SB, 0, 16), rs(o_pscale, 64)],
                   W=[rs(o_ypT, (16 * CS * 2))])
                sample_first[0] = True

            ckpt("P", l, t)
            PT = [sbv(o_C + 12480 + i * 1024, BF16, 512) for i in range(2)]
            R_PT = [rs(o_C + 12480 + i * 1024, 1024) for i in range(2)]
            ET = [sbv(o_C + 14528 + i * 1024, BF16, 512) for i in range(2)]
            R_ET = [rs(o_C + 14528 + i * 1024, 1024) for i in range(2)]
            rz = sbv(o_C + 20672, F32, 512); R_rz = rs(o_C + 20672, 2048)
            klo = [max(0, p0 - 128), max(0, p0 - 512), 0]
            kn = [p0 + T - klo[g] for g in range(3)]
            koff = [0, 640, 640 + 1024]
            pcount = 0
            for h in range(H):
                hb = h % 2
                KTh = sbv(o_KTh + hb * 7424, BF16, 3712); oK = o_KTh + hb * 7424
                Vh = sbv(o_Vh + hb * 7424, BF16, 29, 128); oV = o_Vh + hb * 7424
                for g in range(3):
                    dma("sp", f"kh{hb}{g}", KTh[:, koff[g]:koff[g] + kn[g]], KTs[l, g, :, h, klo[g]:p0 + T],
                        R=[(f"KTs{l}{g}{h}", klo[g], p0 + T)], W=[rs(oK + koff[g] * 2, kn[g] * 2)])
                if True:
                    b0 = max(0, t * 4 - 1)
                    nb1 = t * 4 + 4 - b0
                    dma("sp", f"vh{hb}0", Vh[:, 5 - nb1:5, :], Vs[l, 0, b0:t * 4 + 4, :, h * 128:(h + 1) * 128].rearrange("b p d -> p b d"),
                        R=[(f"Vs{l}0", b0 * 128, (t * 4 + 4) * 128)], W=[rs(oV + (5 - nb1) * 256, nb1 * 256)])
                    b0 = max(0, t * 4 - 4)
                    nb2 = t * 4 + 4 - b0
                    dma("sp", f"vh{hb}1", Vh[:, 13 - nb2:13, :], Vs[l, 1, b0:t * 4 + 4, :, h * 128:(h + 1) * 128].rearrange("b p d -> p b d"),
                        R=[(f"Vs{l}1", b0 * 128, (t * 4 + 4) * 128)], W=[rs(oV + (13 - nb2) * 256, nb2 * 256)])
                    nb3 = 4 * (t + 1)
                    dma("sp", f"vh{hb}2", Vh[:, 13:13 + nb3, :], Vs[l, 2, 0:nb3, :, h * 128:(h + 1) * 128].rearrange("b p d -> p b d"),
                        R=[(f"Vs{l}2", 0, nb3 * 128)], W=[rs(oV + 13 * 256, nb3 * 256)])
                R_K = rs(oK, 7424); R_V = rs(oV, 7424)
                BANKS[:] = [0, 1, 2]
                bU, bZ = (5, 6) if h % 2 == 0 else (7, 3)
                first = [True]

                def pv(pt, R_pt, vblk, nk, out_cols_fn):
                    items = []
                    for (lhsV, cols_ap_u, cols_ap_z, rhs) in out_cols_fn:
                        st = first[0]
                        first[0] = False
                        items.append((cols_ap_u, lhsV, rhs, st, True))
                        items.append((cols_ap_z, ones_bf[0:nk, :], rhs, st, True))
                    mmgroup(items, R=[R_V, R_pt, R_ones], W=[rp(bU), rp(bZ)])

                def softmax_tile(bS, lo, hi, mask_i, nk=128):
                    s_ = pcount_box[0] % 2
                    pcount_box[0] += 1
                    act(ET[s_][0:nk, lo:hi], psum[0:nk, bS * 512 + lo:bS * 512 + hi], AF.Exp, R=[rp(bS, lo, hi)], W=[R_ET[s_]], scale=SCALE)
                    tt("pool", PT[s_][0:nk, lo:hi], ET[s_][0:nk, lo:hi], masks[0:nk, mask_i, lo:hi], ALU.mult,
                       R=[R_ET[s_], rs(o_masks, 4096)], W=[R_PT[s_]])
                    return PT[s_], R_PT[s_]
                pcount_box = [pcount]
                Qg = [QT[:, g, h, :] for g in range(3)]
                R_Q = rs(o_QT, 24576)
                Ub = psum[:, bU * 512:(bU + 1) * 512]
                Zb = psum[:, bZ * 512:(bZ + 1) * 512]
                kb1 = klo[0]
                for diag in range(2):
                    qbs = [qb for qb in range(4) if not (diag == 1 and t == 0 and qb == 0)]
                    bS = nbank()
                    items = []
                    for qb in qbs:
                        kpos = p0 + qb * 128 - diag * 128 - kb1
                        items.append((pb(bS, qb * 128, qb * 128 + 128), KTh[:, koff[0] + kpos:koff[0] + kpos + 128],
                                      Qg[0][:, qb * 128:(qb + 1) * 128], qb == qbs[0], True))
                    lo = qbs[0] * 128
                    mmgroup(items, R=[R_K, R_Q], W=[rp(bS, lo, 512)])
                    pt, R_pt = softmax_tile(bS, lo, 512, diag)
                    lst = []
                    for qb in qbs:
                        vslot = 1 + qb - diag
                        lst.append((Vh[:, vslot, :], Ub[:, qb * 128:(qb + 1) * 128], Zb[:, qb * 128:(qb + 1) * 128],
                                    pt[:, qb * 128:(qb + 1) * 128]))
                    pv(pt, R_pt, None, 128, lst)
                kb2 = klo[1]
                for diag in range(2):
                    if diag == 1 and t == 0:
                        continue
                    bS = nbank()
                    items = []
                    for r in range(4):
                        kbase = koff[1] + (p0 - diag * 512 - kb2)
                        kap = KTh[:, kbase:kbase + 512].rearrange("p (i r) -> p r i", r=4)[:, r, :]
                        qap = Qg[1].rearrange("p (i r) -> p r i", r=4)[:, r, :]
                        items.append((pb(bS, r * 128, r * 128 + 128), kap, qap, r == 0, True))
                    mmgroup(items, R=[R_K, R_Q], W=[rp(bS)])
                    pt, R_pt = softmax_tile(bS, 0, 512, diag)
                    lst = []
                    for r in range(4):
                        vslot = 9 + r - 4 * diag
                        lst.append((Vh[:, vslot, :], Ub.rearrange("p (i r) -> p r i", r=4)[:, r, :],
                                    Zb.rearrange("p (i r) -> p r i", r=4)[:, r, :], pt[:, r * 128:(r + 1) * 128]))
                    pv(pt, R_pt, None, 128, lst)
                for tp in range(t + 1):
                    bS = nbank()
                    items = []
                    for r in range(4):
                        kbase = koff[2] + 512 * tp
                        kap = KTh[:, kbase:kbase + 512].rearrange("p (i r) -> p r i", r=4)[:, r, :]
                        qap = Qg[2].rearrange("p (i r) -> p r i", r=4)[:, r, :]
                        items.append((pb(bS, r * 128, r * 128 + 128), kap, qap, r == 0, True))
                    mmgroup(items, R=[R_K, R_Q], W=[rp(bS)])
                    pt, R_pt = softmax_tile(bS, 0, 512, 2 if tp == t else 3)
                    lst = []
                    for r in range(4):
                        lst.append((Vh[:, 13 + tp * 4 + r, :], Ub.rearrange("p (i r) -> p r i", r=4)[:, r, :],
                                    Zb.rearrange("p (i r) -> p r i", r=4)[:, r, :], pt[:, r * 128:(r + 1) * 128]))
                    pv(pt, R_pt, None, 128, lst)
                pcount = pcount_box[0]
                S.add("dve", lambda e, Zb=Zb: e.reciprocal(out=rz, in_=Zb), R=[rp(bZ)], W=[R_rz])
                tt("dve", oT[:, h, 0:512], Ub, rz, ALU.mult, R=[rp(bU), R_rz], W=[rs(o_oT + h * (CS * 2), (CS * 2))])
                BANKS[:] = ALLB

            ckpt("ATT", l, t)
            sg = [sbv(o_C + 16576 + i * 2048, F32, 512) for i in range(2)]
            R_sg = [rs(o_C + 16576 + i * 2048, 2048) for i in range(2)]
            tm = sbv(o_C + 6144, F32, 512); R_tm = rs(o_C + 6144, 2048)
            tm2 = sbv(o_C + 8256, F32, 512); R_tm2 = rs(o_C + 8256, 2048)
            for m in range(32):
                o1, v1 = w_get(8192)
                o2, v2 = w_get(3072)
                g3_ = v1.rearrange("p (s k c) -> p s k c", s=2, c=128)
                bpv = v2[:, 0:2048].rearrange("p (k c) -> p k c", c=128)
                bav = v2[:, 2048:3072].rearrange("p (k c) -> p k c", c=128)
                bgp = f_chunk(o1, 16384, lambda kc: g3_[:, 0, kc, :], KC, lambda kc, lo, hi: XT[:, kc, lo:hi], [R_XT], 4 * m)
                bga = f_chunk(o1, 16384, lambda kc: g3_[:, 1, kc, :], KC, lambda kc, lo, hi: XT[:, kc, lo:hi], [R_XT], 4 * m + 1)
                bbp = f_chunk(o2, 6144, lambda kc: bpv[:, kc, :], 16, lambda kc, lo, hi: ypT[:, kc, lo:hi], [rs(o_ypT, (16 * CS * 2))], 4 * m + 2)
                bba = f_chunk(o2, 6144, lambda kc: bav[:, kc, :], 8, lambda kc, lo, hi: oT[:, kc, lo:hi], [rs(o_oT, (8 * CS * 2))], 4 * m + 3)
                act(sg[0], pb(bgp), AF.Sigmoid, R=[rp(bgp)], W=[R_sg[0]])
                act(sg[1], pb(bga), AF.Sigmoid, R=[rp(bga)], W=[R_sg[1]])
                tt("dve", tm, sg[0], pb(bbp), ALU.mult, R=[R_sg[0], rp(bbp)], W=[R_tm])
                tt("dve", tm2, sg[1], pb(bba), ALU.mult, R=[R_sg[1], rp(bba)], W=[R_tm2])
                tt("pool", mgT[:, m, 0:512], tm, tm2, ALU.add, R=[R_tm, R_tm2], W=[rs(o_mg + m * (CS * 2), (CS * 2))])
            if t0:
                sv = s2[:, 0:128]; R_s2 = rs(o_s2, 3200)
                act(sv, pb(SB, 0, 128), AF.Copy, R=[rp(SB, 0, 128)], W=[R_s2])
                sv4 = sv.rearrange("p (m f) -> p f m", f=4)
                sgs = s2[:, 128:192].rearrange("p (f m) -> p f m", f=2)
                act(sgs[:, 0, :], sv4[:, 0, :], AF.Sigmoid, R=[R_s2], W=[R_s2])
                act(sgs[:, 1, :], sv4[:, 1, :], AF.Sigmoid, R=[R_s2], W=[R_s2])
                tt("dve", sgs[:, 0, :], sgs[:, 0, :], sv4[:, 2, :], ALU.mult, R=[R_s2], W=[R_s2])
                tt("dve", sgs[:, 1, :], sgs[:, 1, :], sv4[:, 3, :], ALU.mult, R=[R_s2], W=[R_s2])
                tt("dve", mgT[:, :, 512], sgs[:, 0, :], sgs[:, 1, :], ALU.add, R=[R_s2], W=[rs(o_mg, (KC * CS * 2))])
                sample_first[0] = True

            ckpt("M", l, t)
            def t_proj(act_view, R_act, nkc, kcs_list, res_src, res_name, gi, dest_kind):
                xres = sbv(o_A, F32, 2, 4, 512)
                xpst = sbv(o_A + 16384, F32, 2, 4, 512)
                for cg in range(8):
                    par = cg % 2
                    for tb in range(4):
                        r0 = p0 + tb * 128
                        dma("sp", f"xr{par}{tb}", xres[:, par, tb, :], res_src[r0:r0 + 128, cg * 512:(cg + 1) * 512],
                            R=[(f"{res_name}c{cg}", r0, r0 + 128)] if res_name else [],
                            W=[rs(o_A + (par * 4 + tb) * 2048, 2048)])
                    banks = [nbank() for _ in range(4)]
                    kc = 0
                    for nk in kcs_list:
                        boff, bv = w_get(nk * 512)
                        b3 = bv.rearrange("p (k c) -> p k c", c=512)
                        items = []
                        W_ = [rp(bk) for bk in banks]
                        for kk in range(nk):
                            for tb in range(4):
                                items.append((pb(banks[tb]), act_view[:, kc + kk, tb * 128:(tb + 1) * 128], b3[:, kk, :],
                                              kc + kk == 0, kc + kk == nkc - 1))
                        if t0:
                            for sub in range(4):
                                col = cg * 4 + sub
                                its = []
                                for kk in range(nk):
                                    st = sample_first[0]
                                    sample_first[0] = False
                                    its.append((pb(SB, col, col + 1), b3[:, kk, sub * 128:(sub + 1) * 128],
                                                act_view[:, kc + kk, 512:513], st, kc + kk == nkc - 1))
                                items += its
                                W_.append(rp(SB, col, col + 1))
                        mmgroup(items, R=[rs(boff, nk * 1024), R_act], W=W_)
                        kc += nk
                    for tb in range(4):
                        r0 = p0 + tb * 128
                        Rx = rs(o_A + (par * 4 + tb) * 2048, 2048)
                        Rp = rs(o_A + 16384 + (par * 4 + tb) * 2048, 2048)
                        stt(xpst[:, par, tb, :], xres[:, par, tb, :], ALPHA, pb(banks[tb]), ALU.mult, ALU.add,
                            R=[Rx, rp(banks[tb])], W=[Rp])
                        S.add("dve", lambda e, par=par, tb=tb, cg=cg: e.bn_stats(out=stats[:, tb, cg, :], in_=xpst[:, par, tb, :]),
                              R=[Rp], W=[rs(o_stats + (tb * 8 + cg) * 24, 24)])
                        dma("sp", f"xp{par}{tb}", xpre[r0:r0 + 128, cg * 512:(cg + 1) * 512], xpst[:, par, tb, :],
                            R=[Rp], W=[(f"xprec{cg}", r0, r0 + 128)])
                xs = [sbv(o_B + i * 16384, F32, D) for i in range(2)]
                R_xs = [rs(o_B + i * 16384, 16384) for i in range(2)]
                gB = sbv(o_B + 32768, F32, D); bB = sbv(o_B + 49152, F32, D)
                R_gB = rs(o_B + 32768, 16384); R_bB = rs(o_B + 49152, 16384)
                xTn = sbv(o_B + 65536, BF16, KC, 128); R_xTn = rs(o_B + 65536, 8192)
                dma("sp", "gB", gB, lnrow[l, gi:gi + 1, :].partition_broadcast(128), W=[R_gB])
                dma("sp", "bB", bB, lnrow[l, gi + 1:gi + 2, :].partition_broadcast(128), W=[R_bB])
                sm = sbv(o_small, F32, 4, 8); R_sm = rs(o_small, 128)
                def ln_load(tb_):
                    r0_ = p0 + tb_ * 128
                    i_ = tb_ % 2
                    dma("sp", f"xs{i_}", xs[i_], xpre[r0_:r0_ + 128, :], R=[(f"xprec{cg}", r0_, r0_ + 128) for cg in range(8)], W=[R_xs[i_]])
                ln_load(0)
                ln_load(1)
                for tb in range(4):
                    r0 = p0 + tb * 128
                    i = tb % 2
                    S.add("dve", lambda e, tb=tb: e.bn_aggr(out=sm[:, tb, 0:2], in_=stats[:, tb, :, :].rearrange("p a b -> p (a b)")),
                          R=[rs(o_stats + tb * 192, 192)], W=[R_sm])
                    ts("dve", sm[:, tb, 2:3], sm[:, tb, 1:2], EPS, None, ALU.add, ALU.bypass, R=[R_sm], W=[R_sm])
                    act(sm[:, tb, 3:4], sm[:, tb, 2:3], AF.Sqrt, R=[R_sm], W=[R_sm])
                    S.add("dve", lambda e, tb=tb: e.reciprocal(out=sm[:, tb, 4:5], in_=sm[:, tb, 3:4]), R=[R_sm], W=[R_sm])
                    stt(sm[:, tb, 5:6], sm[:, tb, 0:1], -1.0, sm[:, tb, 4:5], ALU.mult, ALU.mult, R=[R_sm], W=[R_sm])
                    act(xs[i], xs[i], AF.Identity, R=[R_xs[i], R_sm], W=[R_xs[i]], bias=sm[:, tb, 5:6], scale=sm[:, tb, 4:5])
                    tt("dve", xs[i], xs[i], gB, ALU.mult, R=[R_xs[i], R_gB], W=[R_xs[i]])
                    Rh0 = rs(o_B + i * 16384, 8192); Rh1 = rs(o_B + i * 16384 + 8192, 8192)
                    tt("dve", xs[i][:, 0:2048], xs[i][:, 0:2048], bB[:, 0:2048], ALU.add, R=[Rh0, R_bB], W=[Rh0])
                    tt("pool", xs[i][:, 2048:4096], xs[i][:, 2048:4096], bB[:, 2048:4096], ALU.add, R=[Rh1, R_bB], W=[Rh1])
                    if dest_kind == "mid":
                        dma("sp", f"xo{i}", xmid[r0:r0 + 128, :], xs[i], R=[R_xs[i]],
                            W=[(f"xmidc{cg}", r0, r0 + 128) for cg in range(8)])
                    elif l == 0 and NL > 1:
                        dma("sp", f"xo{i}", x1row[r0:r0 + 128, :], xs[i], R=[R_xs[i]],
                            W=[(f"x1rowc{cg}", r0, r0 + 128) for cg in range(8)])
                    else:
                        dma("sp", f"xo{i}", y[r0:r0 + 128, :], xs[i], R=[R_xs[i]])
                    if dest_kind == "mid" or (l == 0 and NL > 1):
                        for q4 in range(8):
                            bt = nbank()

                            def fn(e, bt=bt, q4=q4, i=i):
                                last = None
                                for j in range(4):
                                    k = q4 * 4 + j
                                    last = e.transpose(out=pb(bt, j * 128, (j + 1) * 128), in_=xs[i][:, k * 128:(k + 1) * 128], identity=ident)
                                return last
                            S.add("pe", fn, R=[R_xs[i], R_ident], W=[rp(bt)])
                            if dest_kind == "mid":
                                outv = XT[:, q4 * 4:q4 * 4 + 4, tb * 128:(tb + 1) * 128]
                                Wv = [R_XT]
                            else:
                                outv = xTn[:, q4 * 4:q4 * 4 + 4, :]
                                Wv = [R_xTn]
                            copy_op(ev_eng(), outv, pb(bt).rearrange("p (j t) -> p j t", t=128), R=[rp(bt)], W=Wv)
                        if dest_kind != "mid":
                            dma("sp", "xTn", xT1[:, :, r0:r0 + 128], xTn, R=[R_xTn], W=[("xT1", r0, r0 + 128)])
                    if tb + 2 < 4:
                        ln_load(tb + 2)

            def sample_ln(ycols, gi, out_f32, R_out_w, out_bf, W_bf):
                R_s2 = rs(o_s2, 3200)
                xp = s2[:, 0:32]; sq = s2[:, 32:64]
                stt(xp, xsres, ALPHA, ycols, ALU.mult, ALU.add, R=[rs(o_xsres, 128), rp(SB, 0, 32)], W=[R_s2])
                tt("dve", sq, xp, xp, ALU.mult, R=[R_s2], W=[R_s2])
                xb = sbv(o_samp + 3952, F32, 2); R_xb = rs(o_samp + 3952, 8)
                S.add("dve", lambda e: e.tensor_reduce(out=xb[:, 0:1], in_=xp, axis=AX.X, op=ALU.add), R=[R_s2], W=[R_xb])
                S.add("dve", lambda e: e.tensor_reduce(out=xb[:, 1:2], in_=sq, axis=AX.X, op=ALU.add), R=[R_s2], W=[R_xb])
                onesf = s2[:, 64:192]
                S.add("dve", lambda e: e.memset(onesf, 1.0), W=[R_s2])
                bt = nbank()
                mmgroup([(pb(bt, 0, 2), onesf, xb, True, True)], R=[R_s2, R_xb], W=[rp(bt, 0, 2)])
                st_ = s2[:, 200:210]
                ts("dve", st_[:, 0:2], pb(bt, 0, 2), 1.0 / D, None, ALU.mult, ALU.bypass, R=[rp(bt, 0, 2)], W=[R_s2])
                tt("dve", st_[:, 2:3], st_[:, 0:1], st_[:, 0:1], ALU.mult, R=[R_s2], W=[R_s2])
                tt("dve", st_[:, 3:4], st_[:, 1:2], st_[:, 2:3], ALU.subtract, R=[R_s2], W=[R_s2])
                ts("dve", st_[:, 3:4], st_[:, 3:4], EPS, None, ALU.add, ALU.bypass, R=[R_s2], W=[R_s2])
                act(st_[:, 4:5], st_[:, 3:4], AF.Sqrt, R=[R_s2], W=[R_s2])
                S.add("dve", lambda e: e.reciprocal(out=st_[:, 5:6], in_=st_[:, 4:5]), R=[R_s2], W=[R_s2])
                ts("dve", xp, xp, st_[:, 0:1], st_[:, 5:6], ALU.subtract, ALU.mult, R=[R_s2], W=[R_s2])
                tt("dve", xp, xp, lnfm_t[:, gi, :], ALU.mult, R=[R_s2, rs(o_lnfm, 512)], W=[R_s2])
                tt("dve", out_f32, xp, lnfm_t[:, gi + 1, :], ALU.add, R=[R_s2, rs(o_lnfm, 512)], W=[R_out_w])
                if out_bf is not None:
                    copy_op("dve", out_bf, out_f32, R=[R_out_w], W=W_bf)

            res_src = xrow if l == 0 else x1row
            t_proj(mgT, rs(o_mg, (KC * CS * 2)), KC, [16, 16], res_src, None if l == 0 else "x1row", 0, "mid")
            if t0:
                sample_ln(pb(SB, 0, 32), 0, xsres, rs(o_xsres, 128), XT[:, :, 512], [R_XT])
                sample_first[0] = True

            ckpt("O", l, t)
            ae = sbv(o_C + 6144, F32, 514); R_ae = rs(o_C + 6144, 2056)
            acc = sbv(o_C + 8256, F32, 512); R_acc = rs(o_C + 8256, 2048)
            gl = sbv(o_C + 10368, F32, 512); R_gl = rs(o_C + 10368, 2048)
            for mf in range(FC):
                boff, bv = w_get(SLOT)
                b4 = bv.rearrange("p (s k c) -> p s k c", s=2, c=128)
                ba_ = f_chunk(boff, SLOT * 2, lambda kc: b4[:, 0, kc, :], KC, lambda kc, lo, hi: XT[:, kc, lo:hi], [R_XT], 2 * mf)
                bb_ = f_chunk(boff, SLOT * 2, lambda kc: b4[:, 1, kc, :], KC, lambda kc, lo, hi: XT[:, kc, lo:hi], [R_XT], 2 * mf + 1)
                R_ah = rs(o_ahist + mf * 8, 8)
                copy_op("dve", ae[:, 0:2], ahist[:, mf, :], R=[R_ah], W=[R_ae])
                copy_op("act", ae[:, 2:514], pb(ba_), R=[rp(ba_)], W=[R_ae])
                Rcw = rs(o_convw, 1032); Rcb = rs(o_convb, 344)
                ts("dve", acc, ae[:, 2:514], convw_t[:, 2, mf:mf + 1], convb_t[:, mf:mf + 1], ALU.mult, ALU.add,
                   R=[R_ae, Rcw, Rcb], W=[R_acc])
                stt(acc, ae[:, 1:513], convw_t[:, 1, mf:mf + 1], acc, ALU.mult, ALU.add, R=[R_ae, Rcw, R_acc], W=[R_acc])
                stt(acc, ae[:, 0:512], convw_t[:, 0, mf:mf + 1], acc, ALU.mult, ALU.add, R=[R_ae, Rcw, R_acc], W=[R_acc])
                act(gl, acc, AF.Gelu, R=[R_acc], W=[R_gl])
                tt("dve", hmT[:, mf, 0:512], gl, pb(bb_), ALU.mult, R=[R_gl, rp(bb_)], W=[rs(o_B + mf * (CS * 2), (CS * 2))])
                copy_op("dve", ahist[:, mf, :], ae[:, 512:514], R=[R_ae], W=[R_ah])
            if t0:
                R_s2 = rs(o_s2, 3200)
                ab = s2[:, 0:172]
                act(ab, pb(SB, 0, 172), AF.Copy, R=[rp(SB, 0, 172)], W=[R_s2])
                ab2 = ab.rearrange("p (m f) -> p f m", f=2)
                sc_ = sbv(o_small + 256, F32, 2, FC); R_sc = rs(o_small + 256, 688)
                dma("sp", "c2", sc_, sconv[l], W=[R_sc])
                cacc = s2[:, 180:266]; ctmp = s2[:, 270:356]
                Rcw = rs(o_convw, 1032); Rcb = rs(o_convb, 344)
                tt("dve", cacc, ab2[:, 0, :], convw_t[:, 2, :], ALU.mult, R=[R_s2, Rcw], W=[R_s2])
                tt("dve", cacc, cacc, convb_t, ALU.add, R=[R_s2, Rcb], W=[R_s2])
                tt("dve", ctmp, sc_[:, 1, :], convw_t[:, 1, :], ALU.mult, R=[R_sc, Rcw], W=[R_s2])
                tt("dve", cacc, cacc, ctmp, ALU.add, R=[R_s2], W=[R_s2])
                tt("dve", ctmp, sc_[:, 0, :], convw_t[:, 0, :], ALU.mult, R=[R_sc, Rcw], W=[R_s2])
                tt("dve", cacc, cacc, ctmp, ALU.add, R=[R_s2], W=[R_s2])
                act(ctmp, cacc, AF.Gelu, R=[R_s2], W=[R_s2])
                tt("dve", hmT[:, :, 512], ctmp, ab2[:, 1, :], ALU.mult, R=[R_s2], W=[rs(o_B, (FC * CS * 2))])
                co = s2[:, 360:532].rearrange("p (r c) -> p r c", r=2)
                copy_op("dve", co[:, 0, :], sc_[:, 1, :], R=[R_sc], W=[R_s2])
                copy_op("dve", co[:, 1, :], ab2[:, 0, :], R=[R_s2], W=[R_s2])
                dma("sp", "c0", convs[l], co, R=[R_s2])
                sample_first[0] = True

            ckpt("U", l, t)
            t_proj(hmT, rs(o_B, (FC * CS * 2)), FC, DKC, xmid, "xmid", 2, "out")
            if t0:
                if l == 0 and NL > 1:
                    sample_ln(pb(SB, 0, 32), 2, xsres, rs(o_xsres, 128), None, None)
                else:
                    yo = s2[:, 220:252]
                    sample_ln(pb(SB, 0, 32), 2, yo, rs(o_s2, 3200), None, None)
                    dma("sp", "c1", ysT, yo, R=[rs(o_s2, 3200)])
                sample_first[0] = True

        dma("sp", "c0", poolp[l], uhist, R=[rs(o_uhist, 960)])
        dma("sp", "c1", convp[l], ahist, R=[rs(o_ahist, 688)])

    except StopBuild:
        pass

    eng_cnt = {}
    chan_cnt = {}
    for op in S.ops:
        if op.chan is not None:
            chan_cnt[op.chan] = chan_cnt.get(op.chan, 0) + 16
            op.sig = (op.chan, chan_cnt[op.chan])
        elif op.need:
            eng_cnt[op.eng] = eng_cnt.get(op.eng, 0) + 1
            op.sig = (op.eng, eng_cnt[op.eng])
    print("ops", len(S.ops), "eng sig counts", eng_cnt, "nchan", len(chan_cnt), "max chan", max(chan_cnt.values()), flush=True)

    sems = {}
    for name in list(eng_cnt.keys()) + list(chan_cnt.keys()):
        sems[name] = es.enter_context(nc.semaphore("s_" + name))

    block = es.enter_context(nc.Block())

    def emit(engname):
        def body(e):
            seen = {}
            for op in S.ops:
                if op.eng != engname:
                    continue
                for d in op.deps:
                    sname, val = S.ops[d].sig
                    if seen.get(sname, 0) >= val:
                        continue
                    e.wait_ge(sems[sname], val)
                    seen[sname] = val
                inst = op.fn(e)
                if op.sig is not None:
                    inst.then_inc(sems[op.sig[0]], 16 if op.chan is not None else 1)
                    if op.sig[0] == engname:
                        pass
            if engname == "sp":
                for cname, val in chan_cnt.items():
                    if seen.get(cname, 0) < val:
                        e.wait_ge(sems[cname], val)
        return body

    block.tensor(emit("pe"))
    block.scalar(emit("act"))
    block.vector(emit("dve"))
    block.gpsimd(emit("pool"))
    block.sync(emit("sp"))
    es.close()
    return nc


_NC_CACHE = {}


def prep_inputs(x_prompt, x_sample, state_pool, cache_kv1, cache_kv2, cache_kv3, state_conv,
                w_in, w_pool_grp, pool_scale, w_br_pool, w_br_att, w_out, ln1_g, ln1_b,
                w_up, conv_w, conv_b, w_down, ln2_g, ln2_b, cores=range(8), NL=L):
    f = np.float32
    A = lambda a: np.asarray(a, dtype=f)
    x_prompt = A(x_prompt); x_sample = A(x_sample); state_pool = A(state_pool)
    caches = [A(cache_kv1), A(cache_kv2), A(cache_kv3)]
    state_conv = A(state_conv)
    w_in = A(w_in); w_pool_grp = A(w_pool_grp); pool_scale = A(pool_scale); w_br_pool = A(w_br_pool)
    w_br_att = A(w_br_att); w_out = A(w_out); w_up = A(w_up); conv_w = A(conv_w); conv_b = A(conv_b); w_down = A(w_down)
    ln = [A(ln1_g), A(ln1_b), A(ln2_g), A(ln2_b)]

    ws = np.stack([build_wstream(l, w_in, w_pool_grp, w_br_pool, w_br_att, w_out, w_up, w_down) for l in range(NL)])
    masks, invc, ident = make_consts()
    fm = lambda v, C: np.ascontiguousarray(v.reshape(C, 128).T)
    pscale = np.stack([fm(pool_scale[l], 16) for l in range(L)])
    convw = np.stack([np.stack([fm(conv_w[l, r], FC) for r in range(3)], axis=1) for l in range(L)])
    convb = np.stack([fm(conv_b[l], FC) for l in range(L)])
    lnrow = np.stack([np.stack([ln[i][l] for i in range(4)]) for l in range(L)])
    lnfm = np.stack([np.stack([fm(ln[i][l], KC) for i in range(4)], axis=1) for l in range(L)])

    in_maps = []
    for c in cores:
        b = c // 2
        xb = x_prompt[b]
        xT0 = np.ascontiguousarray(xb.T.reshape(KC, 128, SEQ).transpose(1, 0, 2))
        m = {
            "ws": ws, "xT0": xT0, "xrow": np.ascontiguousarray(xb),
            "xsT": fm(x_sample[c, 0], KC),
            "spool": np.ascontiguousarray(state_pool[:, c].reshape(L, 15, 16, 128).transpose(0, 3, 2, 1)),
            "sconv": np.ascontiguousarray(state_conv[:, c].reshape(L, 2, FC, 128).transpose(0, 3, 1, 2)),
            "pscale": pscale, "convw": convw, "convb": convb, "lnrow": lnrow, "lnfm": lnfm,
            "masks": masks, "invc": invc, "ident": ident,
        }
        for g in range(3):
            m[f"ck{g}"] = np.ascontiguousarray(caches[g][:, c].reshape(L, CLEN[g], 2048))
        in_maps.append(m)
    return in_maps


def kernel(**inputs):
    in_maps = prep_inputs(**inputs)
    if "nc" not in _NC_CACHE:
        _NC_CACHE["nc"] = build_nc()
    nc = _NC_CACHE["nc"]
    res = run_bass_kernel_spmd(nc, in_maps, core_ids=list(range(8))).results

    B = 4
    y_prompt = np.stack([res[2 * b]["y"] for b in range(B)])
    y_sample = np.stack([res[c]["ysT"].T.reshape(1, D) for c in range(8)])
    tofeat = lambda a: a
    pool_p = np.stack([res[2 * b]["poolp"].transpose(0, 3, 2, 1).reshape(L, 15, 2048) for b in range(B)], axis=1)
    pool_s = np.stack([res[c]["pools"].transpose(0, 3, 2, 1).reshape(L, 15, 2048) for c in range(8)], axis=1)
    conv_p = np.stack([res[2 * b]["convp"].transpose(0, 3, 2, 1).reshape(L, 2, DFF) for b in range(B)], axis=1)
    conv_s = np.stack([res[c]["convs"].transpose(0, 2, 3, 1).reshape(L, 2, DFF) for c in range(8)], axis=1)

    def kv_p(b, g):
        r = res[2 * b]
        K = r["kT"][:, g]
        K = K.transpose(0, 3, 2, 1)
        V = r["vo"][:, g]
        if g == 0:
            Vn = V.reshape(L, SEQ, 1024)
        else:
            Vn = V.reshape(L, 4, 4, 128, 1024).transpose(0, 1, 3, 2, 4).reshape(L, SEQ, 1024)
        Vn = Vn.reshape(L, SEQ, H, DH)
        kvf = np.stack([K, Vn], axis=2)
        win = CLEN[g]
        return kvf[:, SEQ - min(win, SEQ):]
    kvp = [np.stack([kv_p(b, g) for b in range(B)], axis=1) for g in range(3)]
    kvs_ = [np.stack([res[c][f"kvs{g}"].reshape(L, CLEN[g], 2, H, DH) for c in range(8)], axis=1) for g in range(3)]
    outs = (y_prompt, y_sample, pool_p, pool_s, kvp[0], kvs_[0], kvp[1], kvs_[1], kvp[2], kvs_[2], conv_p, conv_s)
    return tuple(np.ascontiguousarray(o, dtype=np.float32) for o in outs)


if __name__ == "__main__":
    import time
    t0 = time.time()
    build_nc()
    print("build ok", time.time() - t0)
```

```python
import numpy as np
from contextlib import ExitStack
import concourse.bass as bass
import concourse.mybir as mybir
from concourse.bass_utils import run_bass_kernel_spmd
import ml_dtypes

F32 = mybir.dt.float32
BF16 = mybir.dt.bfloat16
AF = mybir.ActivationFunctionType
ALU = mybir.AluOpType
AX = mybir.AxisListType

D = 4096; KC = 32; T = 512; NT = 4; SEQ = 2048; DFF = 11008; FC = 86; H = 8; DH = 128
L = 2; GATE_OFF = 2048 + 9216
CS = 514
ALPHA = (2.0 * L) ** 0.25; EPS = 1e-5; SCALE = DH ** -0.5
DILS = (1, 4, 16); CLEN = (128, 512, 2048); PWIN = (2, 4, 8, 16)
SLOT = 8192; NSLOT = 3
DKC = [16, 16, 16, 16, 16, 6]


def wplan():
    p = []
    p += [SLOT] * 32
    p += [SLOT] * 12
    p += [2048] * 4
    for _ in range(32):
        p += [8192, 3072]
    p += [SLOT] * 16
    p += [SLOT] * FC
    for _ in range(8):
        p += [k * 512 for k in DKC]
    return p


def _blk(W, r0, nk, c0, nc_):
    a = W[r0:r0 + nk * 128, c0:c0 + nc_]
    return np.ascontiguousarray(a.reshape(nk, 128, nc_).transpose(1, 0, 2)).reshape(128, nk * nc_)


def a_chunk_cols():
    cols = [c * 128 for c in range(16)]
    for g in range(3):
        base = 2048 + 3 * g * 1024
        cols += [base + h * 128 for h in range(8)]
        cols += [base + 1024 + h * 128 for h in range(8)]
    return cols


def build_wstream(l, w_in, w_pool_grp, w_br_pool, w_br_att, w_out, w_up, w_down):
    out = []
    cols = a_chunk_cols()
    for i in range(32):
        assert cols[2 * i + 1] == cols[2 * i] + 128
        out.append(_blk(w_in[l], 0, 32, cols[2 * i], 256))
    for g in range(3):
        for half in range(2):
            c0 = 2048 + 3 * g * 1024 + 2048 + half * 512
            for kb in range(2):
                out.append(_blk(w_in[l], kb * 2048, 16, c0, 512))
    for g in range(4):
        out.append(_blk(w_pool_grp[l, g], 0, 4, 0, 512))
    for m in range(32):
        out.append(np.concatenate([_blk(w_in[l], 0, 32, GATE_OFF + m * 128, 128),
                                   _blk(w_in[l], 0, 32, GATE_OFF + D + m * 128, 128)], axis=1))
        out.append(np.concatenate([_blk(w_br_pool[l], 0, 16, m * 128, 128),
                                   _blk(w_br_att[l], 0, 8, m * 128, 128)], axis=1))
    for cg in range(8):
        for kb in range(2):
            out.append(_blk(w_out[l], kb * 2048, 16, cg * 512, 512))
    for mf in range(FC):
        out.append(np.concatenate([_blk(w_up[l], 0, 32, mf * 128, 128),
                                   _blk(w_up[l], 0, 32, DFF + mf * 128, 128)], axis=1))
    for cg in range(8):
        k0 = 0
        for nk in DKC:
            out.append(_blk(w_down[l], k0 * 128, nk, cg * 512, 512))
            k0 += nk
    sizes = [o.shape[1] for o in out]
    assert sizes == wplan(), "plan mismatch"
    return np.concatenate(out, axis=1)


def make_consts():
    k = np.arange(128)[:, None]
    q = np.arange(128)[None, :]
    cur = np.tile((k <= q).astype(np.float32), (1, 4))
    prev = np.tile((k >= q).astype(np.float32), (1, 4))
    same = ((k % 4) == (q % 4)).astype(np.float32)
    m3cur = np.tile(same * (k <= q), (1, 4))
    m3hist = np.tile(same, (1, 4))
    masks = [cur, prev, m3cur, m3hist]
    masks = np.stack(masks, axis=1).astype(ml_dtypes.bfloat16)
    invc = np.zeros((128, 4, 15), np.float32)
    for wi, w in enumerate(PWIN):
        for t in range(15):
            invc[:, wi, t] = 1.0 / min(t + 1, w)
    ident = np.eye(128, dtype=np.float32)
    return masks, invc, ident


class Op:
    __slots__ = ("eng", "fn", "deps", "chan", "sig", "need", "stream")


class Sched:
    def __init__(self):
        self.ops = []
        self.acc = {}
        self.chan_last = {}

    def add(self, eng, fn, R=(), W=(), chan=None):
        idx = len(self.ops)
        op = Op()
        op.eng = eng; op.fn = fn; op.chan = chan; op.sig = None; op.need = chan is not None
        op.stream = chan if chan is not None else eng
        deps = set()
        for (sp, lo, hi) in R:
            for e in self.acc.get(sp, ()):
                if (e[3] or (sp == "ps" and e[4] != op.stream)) and e[0] < hi and lo < e[1]:
                    deps.add(e[2])
        for (sp, lo, hi) in W:
            for e in self.acc.get(sp, ()):
                if e[0] < hi and lo < e[1]:
                    deps.add(e[2])
        if chan is not None and chan in self.chan_last:
            deps.add(self.chan_last[chan])
        if chan is not None:
            self.chan_last[chan] = idx
        for (sp, lo, hi) in W:
            Lst = self.acc.setdefault(sp, [])
            Lst[:] = [e for e in Lst if not (lo <= e[0] and e[1] <= hi)]
            Lst.append([lo, hi, idx, True, op.stream])
        for (sp, lo, hi) in R:
            Lst = self.acc.setdefault(sp, [])
            Lst[:] = [e for e in Lst if not ((not e[3]) and e[4] == op.stream and lo <= e[0] and e[1] <= hi)]
            Lst.append([lo, hi, idx, False, op.stream])
        best = {}
        for d in deps:
            o = self.ops[d]
            if o.stream == "pe" and eng == "pe" and chan is None:
                continue
            if o.stream not in best or best[o.stream] < d:
                best[o.stream] = d
        op.deps = sorted(best.values())
        for d in op.deps:
            self.ops[d].need = True
        self.ops.append(op)
        return idx


class StopBuild(Exception):
    pass


def build_nc(NL=L, NTR=NT, stop=None, ws_cols=None, skip=()):
    nc = bass.Bass("TRN2", target_bir_lowering=False)
    plan = wplan()
    WTOT = sum(plan)
    woffs = np.concatenate([[0], np.cumsum(plan)]).tolist()

    def din(name, shape, dt=F32):
        return nc.dram_tensor(name, list(shape), dt, kind="ExternalInput").ap()

    def dout(name, shape, dt=F32):
        return nc.dram_tensor(name, list(shape), dt, kind="ExternalOutput").ap()

    ws = din("ws", [NL, 128, WTOT if ws_cols is None else ws_cols])
    xT0 = din("xT0", [128, KC, SEQ])
    xrow = din("xrow", [SEQ, D])
    xsT = din("xsT", [128, KC])
    spool = din("spool", [L, 128, 16, 15])
    sconv = din("sconv", [L, 128, 2, FC])
    ck = [din(f"ck{g}", [L, CLEN[g], 2048]) for g in range(3)]
    pscale = din("pscale", [L, 128, 16])
    convw = din("convw", [L, 128, 3, FC])
    convb = din("convb", [L, 128, FC])
    lnrow = din("lnrow", [L, 4, D])
    lnfm = din("lnfm", [L, 128, 4, KC])
    masks_d = din("masks", [128, 4, 512], BF16)
    invc_d = din("invc", [128, 4, 15])
    ident_d = din("ident", [128, 128])

    y = dout("y", [SEQ, D])
    ysT = dout("ysT", [128, KC])
    poolp = dout("poolp", [L, 128, 16, 15])
    pools = dout("pools", [L, 128, 16, 15])
    kT_o = dout("kT", [L, 3, 128, H, SEQ])
    v_o = dout("vo", [L, 3, 16, 128, 1024])
    kvs = [dout(f"kvs{g}", [L, CLEN[g], 2048]) for g in range(3)]
    convp = dout("convp", [L, 128, FC, 2])
    convs = dout("convs", [L, 128, 2, FC])

    xpre = nc.dram_tensor("xpre", [SEQ, D], F32).ap()
    xmid = nc.dram_tensor("xmid", [SEQ, D], F32).ap()
    x1row = nc.dram_tensor("x1row", [SEQ, D], F32).ap()
    xT1 = nc.dram_tensor("xT1", [128, KC, SEQ], BF16).ap()
    KTs = nc.dram_tensor("KTs", [L, 3, 128, H, SEQ], BF16).ap()
    Vs = nc.dram_tensor("Vs", [L, 3, 16, 128, 1024], BF16).ap()

    es = ExitStack()
    SBTOT = 212800
    arena = es.enter_context(nc.sbuf_tensor("arena", [128, SBTOT // 2], BF16))
    psum = es.enter_context(nc.psum_tensor("psum", [128, 4096], F32))

    def sbv(off, dt, *dims):
        n = int(np.prod(dims))
        esz = 4 if dt == F32 else 2
        assert off % 4 == 0
        a = arena[:, off // 2: off // 2 + n * esz // 2]
        if dt == F32:
            a = a.bitcast(F32)
        if len(dims) == 2:
            a = a.rearrange("p (a b) -> p a b", b=dims[1])
        elif len(dims) == 3:
            a = a.rearrange("p (a b c) -> p a b c", b=dims[1], c=dims[2])
        return a

    def rs(off, nbytes):
        return ("sb", off, off + nbytes)

    def rp(bank, lo=0, hi=512):
        return ("ps", bank * 2048, bank * 2048 + 2048)

    def pb(bank, lo=0, hi=512):
        return psum[:, bank * 512 + lo: bank * 512 + hi]

    o_ident = 0; o_ones = 512; o_masks = 1024; o_invc = 7168
    o_pscale = 7680; o_convw = 7744; o_convb = 8776; o_lnfm = 9120
    o_uhist = 10240; o_ahist = 11264; o_stats = 12032; o_small = 12800; o_samp = 14848
    o_W = 19456
    o_A = o_W + NSLOT * SLOT * 2
    o_B = o_A + (KC * CS * 2)
    o_C = o_B + 88448
    assert o_C + 22720 <= SBTOT

    ident = sbv(o_ident, F32, 128)
    ones_bf = sbv(o_ones, BF16, 128)
    masks = sbv(o_masks, BF16, 4, 512)
    invc = sbv(o_invc, F32, 4, 15)
    pscale_t = sbv(o_pscale, F32, 16)
    convw_t = sbv(o_convw, F32, 3, FC)
    convb_t = sbv(o_convb, F32, FC)
    lnfm_t = sbv(o_lnfm, F32, 4, KC)
    uhist = sbv(o_uhist, F32, 16, 15)
    ahist = sbv(o_ahist, F32, FC, 2)
    stats = sbv(o_stats, F32, 4, 8, 6)
    XT = sbv(o_A, BF16, KC, CS)
    o_dT = o_B; o_ypT = o_B + (16 * CS * 2); o_oT = o_B + (KC * CS * 2); o_QT = o_B + 41120
    o_KTh = o_B + 65696; o_Vh = o_B
    o_mg = o_B + 41120
    o_sc = o_B + 74016
    dT = sbv(o_dT, BF16, 16, CS)
    ypT = sbv(o_ypT, BF16, 16, CS)
    oT = sbv(o_oT, BF16, 8, CS)
    QT = sbv(o_QT, BF16, 3, 8, 512)
    mgT = sbv(o_mg, BF16, KC, CS)
    hmT = sbv(o_B, BF16, FC, CS)
    o_xsres = o_samp; o_hs = o_samp + 128; o_s2 = o_samp + 128 + 704
    xsres = sbv(o_xsres, F32, KC)
    hs = sbv(o_hs, F32, 176)
    s2 = sbv(o_s2, F32, 800)

    S = Sched()
    bank_ctr = [0]
    BANKS = [0, 1, 2, 3, 5, 6, 7]

    def nbank():
        b = BANKS[bank_ctr[0] % len(BANKS)]
        bank_ctr[0] += 1
        return b
    ALLB = [0, 1, 2, 3, 5, 6, 7]
    SB = 4

    wstate = {"issued": 0, "next": 0}
    TOTBLK = len(plan) * NL * NTR

    def w_issue(i):
        l = i // (len(plan) * NTR)
        j = i % len(plan)
        n = plan[j]
        slot = i % NSLOT
        dst = sbv(o_W + slot * SLOT * 2, BF16, n)
        src = ws[l, :, woffs[j]: woffs[j] + n]
        S.add("pool", lambda e, dst=dst, src=src: e.dma_start(out=dst, in_=src),
              W=[rs(o_W + slot * SLOT * 2, n * 2)], chan=f"w{slot}")

    def w_get(n):
        i = wstate["next"]
        assert plan[i % len(plan)] == n, (i, plan[i % len(plan)], n)
        while wstate["issued"] < min(TOTBLK, i + NSLOT - 1):
            w_issue(wstate["issued"])
            wstate["issued"] += 1
        wstate["next"] += 1
        slot = i % NSLOT
        off = o_W + slot * SLOT * 2
        return off, sbv(off, BF16, n)

    def dma(eng, chan, out, in_, R=(), W=(), nonc=False):
        if nonc:
            S.add(eng, lambda e: e.dma_start(out=out, in_=in_, allow_slow_non_contiguous=True), R=R, W=W, chan=chan)
        else:
            S.add(eng, lambda e: e.dma_start(out=out, in_=in_), R=R, W=W, chan=chan)

    import sys as _sys
    DBGMAP = build_nc.dbgmap = {}

    def mmgroup(items, R, W):
        lab = (_sys._getframe(1).f_lineno, _sys._getframe(2).f_lineno)

        def fn(e):
            last = None
            n0 = nc.get_next_instruction_name()
            for ii, (o, a, b, st, sp) in enumerate(items):
                last = e.matmul(o, lhsT=a, rhs=b, start=st, stop=sp, skip_group_check=True)
            DBGMAP[(n0, nc.get_next_instruction_name())] = (lab, len(items))
            return last
        S.add("pe", fn, R=R, W=W)

    evq = [0]

    def ev_eng():
        evq[0] += 1
        return "act" if evq[0] % 2 else "dve"

    def copy_op(eng, out, in_, R, W):
        if eng == "act":
            S.add("act", lambda e: e.activation(out=out, in_=in_, func=AF.Copy), R=R, W=W)
        else:
            S.add(eng, lambda e: e.tensor_copy(out=out, in_=in_), R=R, W=W)

    def tt(eng, out, in0, in1, op, R, W):
        S.add(eng, lambda e: e.tensor_tensor(out=out, in0=in0, in1=in1, op=op), R=R, W=W)

    def ts(eng, out, in0, s1, s2_, op0, op1, R, W):
        if s2_ is None:
            S.add(eng, lambda e: e.tensor_scalar(out=out, in0=in0, scalar1=s1, scalar2=None, op0=op0), R=R, W=W)
        else:
            S.add(eng, lambda e: e.tensor_scalar(out=out, in0=in0, scalar1=s1, scalar2=s2_, op0=op0, op1=op1), R=R, W=W)

    def stt(out, in0, sc, in1, op0, op1, R, W):
        S.add("dve", lambda e: e.scalar_tensor_tensor(out=out, in0=in0, scalar=sc, in1=in1, op0=op0, op1=op1), R=R, W=W)

    def act(out, in_, func, R, W, bias=None, scale=None):
        kw = {}
        if bias is not None:
            kw["bias"] = bias
        if scale is not None:
            kw["scale"] = scale
        S.add("act", lambda e: e.activation(out=out, in_=in_, func=func, **kw), R=R, W=W)

    for c0_ in range(0, SBTOT // 2, 16384):
        c1_ = min(SBTOT // 2, c0_ + 16384)
        S.add("dve", lambda e, c0_=c0_, c1_=c1_: e.memset(arena[:, c0_:c1_], 0.0), W=[rs(c0_ * 2, c1_ * 2 - c0_ * 2)])
    dma("sp", "c0", ident, ident_d, W=[rs(o_ident, 512)])
    dma("sp", "c1", masks, masks_d, W=[rs(o_masks, 4096)])
    dma("sp", "c2", invc, invc_d, W=[rs(o_invc, 240)])
    S.add("dve", lambda e: e.memset(ones_bf, 1.0), W=[rs(o_ones, 256)])
    R_ident = rs(o_ident, 512); R_ones = rs(o_ones, 256)

    sample_first = [True]

    def sample_mm_items(col, lhs_list, rhs_list):
        items = []
        n = len(lhs_list)
        for i in range(n):
            st = sample_first[0]
            sample_first[0] = False
            items.append((pb(SB, col, col + 1), lhs_list[i], rhs_list[i], st, i == n - 1))
        return items

    def ckpt(name, l, t):
        if stop is not None and stop == (name, l, t):
            raise StopBuild()

    try:
      for l in range(NL):
        dma("sp", "c0", pscale_t, pscale[l], W=[rs(o_pscale, 64)])
        dma("sp", "c1", convw_t, convw[l], W=[rs(o_convw, 1032)])
        dma("sp", "c2", convb_t, convb[l], W=[rs(o_convb, 344)])
        dma("sp", "c0", lnfm_t, lnfm[l], W=[rs(o_lnfm, 512)])
        S.add("dve", lambda e: e.memset(uhist, 0.0), W=[rs(o_uhist, 960)])
        S.add("dve", lambda e: e.memset(ahist, 0.0), W=[rs(o_ahist, 688)])

        for t in range(NTR):
            p0 = t * T
            t0 = (t == 0)
            NCOL = 513 if t0 else 512
            R_XT = rs(o_A, (KC * CS * 2))

            if l == 0:
                dma("pool", "x0", XT[:, :, 0:512], xT0[:, :, p0:p0 + T], W=[R_XT])
            else:
                dma("sp", "x0", XT[:, :, 0:512], xT1[:, :, p0:p0 + T],
                    R=[("xT1", p0, p0 + T)], W=[R_XT])
            if t0:
                if l == 0:
                    dma("sp", "c1", xsres, xsT, W=[rs(o_xsres, 128)])
                copy_op("dve", XT[:, :, 512], xsres, R=[rs(o_xsres, 128)], W=[R_XT])
                sample_first[0] = True

            ckpt("P0", l, t)
            def f_chunk(blk_off, nbytes, lhs_fn, nk, rhs_fn, R_act, scol):
                b = nbank()
                items = [(pb(b), lhs_fn(kc), rhs_fn(kc, 0, 512), kc == 0, kc == nk - 1) for kc in range(nk)]
                W_ = [rp(b)]
                if t0 and "sample" not in skip:
                    items += sample_mm_items(scol, [lhs_fn(kc) for kc in range(nk)],
                                             [rhs_fn(kc, 512, 513) for kc in range(nk)])
                    W_.append(rp(SB, scol, scol + 1))
                mmgroup(items, R=[rs(blk_off, nbytes)] + R_act, W=W_)
                return b

            ue = sbv(o_C + 6144, F32, 527); tA = sbv(o_C + 8256, F32, 527); tB = sbv(o_C + 10368, F32, 527)
            R_ue = rs(o_C + 6144, 2108); R_tA = rs(o_C + 8256, 2108); R_tB = rs(o_C + 10368, 2108)
            kst = [sbv(o_C + i * 2048, F32, 512) for i in range(2)]
            kbs = [sbv(o_C + 4096 + i * 1024, BF16, 512) for i in range(2)]
            kcount = 0
            for bi in range(32):
                boff, bv = w_get(SLOT)
                b3 = bv.rearrange("p (k c) -> p k c", c=256)
                for sub in range(2):
                    ci = bi * 2 + sub
                    bank = f_chunk(boff, SLOT * 2, lambda kc, b3=b3, sub=sub: b3[:, kc, sub * 128:(sub + 1) * 128], KC,
                                   lambda kc, lo, hi: XT[:, kc, lo:hi], [R_XT], ci)
                    if ci < 16 and "pool" in skip:
                        pass
                    elif ci >= 16 and "qk" in skip:
                        pass
                    elif ci < 16:
                        c = ci
                        wi = c // 4
                        w = PWIN[wi]
                        R_uh = rs(o_uhist + c * 60, 60)
                        copy_op("dve", ue[:, 0:15], uhist[:, c, :], R=[R_uh], W=[R_ue])
                        copy_op("act", ue[:, 15:527], pb(bank), R=[rp(bank)], W=[R_ue])
                        src, Rsrc = ue, R_ue
                        bufs = [(tA, R_tA), (tB, R_tB)]
                        sh = 1
                        k = 0
                        while sh < w:
                            dst, Rdst = bufs[k % 2]
                            tt("dve", dst[:, sh:527], src[:, sh:527], src[:, 0:527 - sh], ALU.add, R=[Rsrc], W=[Rdst])
                            src, Rsrc = dst, Rdst
                            sh *= 2
                            k += 1
                        R_d = rs(o_dT + c * (CS * 2), (CS * 2))
                        stt(dT[:, c, 0:512], src[:, 15:527], 1.0 / w, ue[:, 15:527], ALU.mult, ALU.subtract,
                            R=[Rsrc, R_ue], W=[R_d])
                        if t0:
                            dst, Rdst = bufs[k % 2]
                            tt("dve", dst[:, 0:15], src[:, 15:30], invc[:, wi, :], ALU.mult, R=[Rsrc, rs(o_invc, 240)], W=[Rdst])
                            tt("dve", dT[:, c, 0:15], dst[:, 0:15], ue[:, 15:30], ALU.subtract, R=[Rdst, R_ue], W=[R_d])
                        copy_op("dve", uhist[:, c, :], ue[:, 512:527], R=[R_ue], W=[R_uh])
                    else:
                        j = ci - 16
                        g = j // 16
                        isk = (j % 16) >= 8
                        h = j % 8
                        if not isk:
                            copy_op(ev_eng(), QT[:, g, h, :], pb(bank), R=[rp(bank)],
                                    W=[rs(o_QT + (g * 8 + h) * 1024, 1024)])
                        else:
                            s_ = kcount % 2
                            kcount += 1
                            copy_op("act", kst[s_], pb(bank), R=[rp(bank)], W=[rs(o_C + s_ * 2048, 2048)])
                            copy_op("dve", kbs[s_], kst[s_], R=[rs(o_C + s_ * 2048, 2048)], W=[rs(o_C + 4096 + s_ * 1024, 1024)])
                            dma("sp", f"kst{s_}", kT_o[l, g, :, h, p0:p0 + T], kst[s_], R=[rs(o_C + s_ * 2048, 2048)])
                            if "kts" not in skip:
                              dma("sp", f"kbs{s_}", KTs[l, g, :, h, p0:p0 + T], kbs[s_],
                                  R=[rs(o_C + 4096 + s_ * 1024, 1024)], W=[(f"KTs{l}{g}{h}", p0, p0 + T)])

            ckpt("A", l, t)
            vst = [sbv(o_C + 12480 + i * 2048, F32, 512) for i in range(2)]
            vbs = [sbv(o_C + 16576 + i * 1024, BF16, 512) for i in range(2)]
            vcount = 0
            for g in range(3):
                for half in range(2):
                    blks = [w_get(SLOT), w_get(SLOT)]
                    banks = [nbank() for _ in range(4)]
                    for kb in range(2):
                        boff, bv = blks[kb]
                        b3 = bv.rearrange("p (k c) -> p k c", c=512)
                        items = []
                        for kk in range(16):
                            kc = kb * 16 + kk
                            for tb in range(4):
                                if g == 0:
                                    lhs = XT[:, kc, tb * 128:(tb + 1) * 128]
                                else:
                                    lhs = XT[:, kc, 0:512].rearrange("p (i r) -> p r i", r=4)[:, tb, :]
                                items.append((pb(banks[tb]), lhs, b3[:, kk, :], kc == 0, kc == KC - 1))
                        W_ = [rp(bk) for bk in banks]
                        if t0:
                            for sub in range(4):
                                hcol = 64 + g * 8 + half * 4 + sub
                                items += sample_mm_items(hcol, [b3[:, kk, sub * 128:(sub + 1) * 128] for kk in range(16)],
                                                         [XT[:, kb * 16 + kk, 512:513] for kk in range(16)])
                                W_.append(rp(SB, hcol, hcol + 1))
                        mmgroup(items, R=[rs(boff, SLOT * 2), R_XT], W=W_)
                    for tb in range(4):
                        s_ = vcount % 2
                        vcount += 1
                        Rv = rs(o_C + 12480 + s_ * 2048, 2048); Rb = rs(o_C + 16576 + s_ * 1024, 1024)
                        copy_op("act", vst[s_], pb(banks[tb]), R=[rp(banks[tb])], W=[Rv])
                        copy_op("dve", vbs[s_], vst[s_], R=[Rv], W=[Rb])
                        c0 = half * 512
                        blk_i = t * 4 + tb
                        dma("sp", f"vst{s_}", v_o[l, g, blk_i, :, c0:c0 + 512], vst[s_], R=[Rv])
                        dma("sp", f"vbs{s_}", Vs[l, g, blk_i, :, c0:c0 + 512], vbs[s_], R=[Rb],
                            W=[(f"Vs{l}{g}", blk_i * 128, blk_i * 128 + 128)])

            ckpt("V", l, t)
            if t0:
                copy_op("dve", hs[:, 0:88], pb(SB, 0, 88), R=[rp(SB, 0, 88)], W=[rs(o_hs, 352)])
                R_hs = rs(o_hs, 352)
                for g in range(3):
                    Lc = CLEN[g]
                    dma("sp", f"cc{g}", kvs[g][l, 0:Lc - 1, :], ck[g][l, 1:Lc, :])
                    dma("sp", f"cn{g}", kvs[g][l, Lc - 1, 0:1024].rearrange("(h d) -> d h", d=128),
                        hs[:, 16 + 16 * g + 8:16 + 16 * g + 16], R=[R_hs], nonc=True)
                    dma("sp", f"cn{g}", kvs[g][l, Lc - 1, 1024:2048].rearrange("(h d) -> d h", d=128),
                        hs[:, 64 + 8 * g:64 + 8 * g + 8], R=[R_hs], nonc=True)
                sp_h = sbv(o_sc, F32, 16, 15)
                R_sph = rs(o_sc, 960)
                dma("sp", "c2", sp_h, spool[l], W=[R_sph])
                red = s2[:, 0:16]; R_s2 = rs(o_s2, 3200)
                for wi, w in enumerate(PWIN):
                    S.add("dve", lambda e, wi=wi, w=w: e.tensor_reduce(
                        out=red[:, 4 * wi:4 * wi + 4], in_=sp_h[:, 4 * wi:4 * wi + 4, 15 - (w - 1):15], axis=AX.X, op=ALU.add),
                        R=[R_sph], W=[R_s2])
                    tt("dve", red[:, 4 * wi:4 * wi + 4], red[:, 4 * wi:4 * wi + 4], hs[:, 4 * wi:4 * wi + 4], ALU.add,
                       R=[R_s2, R_hs], W=[R_s2])
                    stt(dT[:, 4 * wi:4 * wi + 4, 512], red[:, 4 * wi:4 * wi + 4], 1.0 / w, hs[:, 4 * wi:4 * wi + 4],
                        ALU.mult, ALU.subtract, R=[R_s2, R_hs], W=[rs(o_dT, (16 * CS * 2))])
                po = s2[:, 16:16 + 240].rearrange("p (c r) -> p c r", r=15)
                copy_op("dve", po[:, :, 0:14], sp_h[:, :, 1:15], R=[R_sph], W=[R_s2])
                copy_op("dve", po[:, :, 14], hs[:, 0:16], R=[R_hs], W=[R_s2])
                dma("sp", "c0", pools[l], po, R=[R_s2])

                ckf = sbv(o_sc + 1024, F32, 2048)
                R_ckf = rs(o_sc + 1024, 8192)
                vb = sbv(o_sc + 1024 + 8192, BF16, 1024); R_vb = rs(o_sc + 9216, 2048)
                kts = sbv(o_sc + 11264, BF16, 8, 128); R_kts = rs(o_sc + 11264, 2048)
                qb_ = s2[:, 300:324]
                qbf = sbv(o_samp + 3840, BF16, 24); R_qbf = rs(o_samp + 3840, 48)
                for g in range(3):
                    copy_op("dve", qbf[:, 8 * g:8 * g + 8], hs[:, 16 + 16 * g:16 + 16 * g + 8], R=[R_hs], W=[R_qbf])
                prod = sbv(o_samp + 3888, BF16, 24); R_prod = rs(o_samp + 3888, 48)
                for g in range(3):
                    tt("dve", prod[:, 8 * g:8 * g + 8], hs[:, 16 + 16 * g:16 + 16 * g + 8],
                       hs[:, 16 + 16 * g + 8:16 + 16 * g + 16], ALU.mult, R=[R_hs], W=[R_prod])
                BANKS[:] = [0, 1, 2, 3]
                bS, bU, bZ = 5, 6, 7
                mmgroup([(pb(bZ, 100, 124), ones_bf, prod, True, True)], R=[R_ones, R_prod], W=[rp(bZ, 100, 124)])
                p0e = s2[:, 330:354]
                act(p0e, pb(bZ, 100, 124), AF.Exp, R=[rp(bZ, 100, 124)], W=[R_s2], scale=SCALE)
                first_u = True
                for g in range(3):
                    dil = DILS[g]
                    src = ck[g][l].rearrange("(i r) f -> r i f", r=dil)[0]
                    dma("sp", "ckf", ckf, src, W=[R_ckf])
                    copy_op("act", vb, ckf[:, 1024:2048], R=[R_ckf], W=[R_vb])
                    for h in range(8):
                        bt = nbank()
                        S.add("pe", lambda e, bt=bt, h=h: e.transpose(out=pb(bt, 0, 128), in_=ckf[:, h * 128:(h + 1) * 128], identity=ident),
                              R=[R_ckf, R_ident], W=[rp(bt, 0, 128)])
                        copy_op(ev_eng(), kts[:, h, :], pb(bt, 0, 128), R=[rp(bt, 0, 128)], W=[R_kts])
                    col = g * 8
                    mmgroup([(pb(bS, col + h, col + h + 1), kts[:, h, :], qbf[:, col + h:col + h + 1], (h == 0), True) for h in range(8)],
                            R=[R_kts, R_qbf], W=[rp(bS, col, col + 8)])
                    pg = sbv(o_samp + 3936, BF16, 8); R_pg = rs(o_samp + 3936, 16)
                    act(pg, pb(bS, col, col + 8), AF.Exp, R=[rp(bS, col, col + 8)], W=[R_pg], scale=SCALE)
                    items = []
                    for h in range(8):
                        items.append((pb(bU, col + h, col + h + 1), vb[:, h * 128:(h + 1) * 128], pg[:, h:h + 1], first_u, True))
                        first_u = False
                    items.append((pb(bZ, col, col + 8), ones_bf, pg, True if g == 0 else False, True))
                    mmgroup(items, R=[R_vb, R_pg, R_ones], W=[rp(bU, col, col + 8), rp(bZ, col, col + 8)])
                Us = s2[:, 360:384]; Zs = s2[:, 390:414]
                tt("dve", Us, p0e, hs[:, 64:88], ALU.mult, R=[R_s2, R_hs], W=[R_s2])
                tt("dve", Us, Us, pb(bU, 0, 24), ALU.add, R=[R_s2, rp(bU, 0, 24)], W=[R_s2])
                tt("dve", Zs, p0e, pb(bZ, 0, 24), ALU.add, R=[R_s2, rp(bZ, 0, 24)], W=[R_s2])
                tt("dve", Us[:, 0:8], Us[:, 0:8], Us[:, 8:16], ALU.add, R=[R_s2], W=[R_s2])
                tt("dve", Us[:, 0:8], Us[:, 0:8], Us[:, 16:24], ALU.add, R=[R_s2], W=[R_s2])
                tt("dve", Zs[:, 0:8], Zs[:, 0:8], Zs[:, 8:16], ALU.add, R=[R_s2], W=[R_s2])
                tt("dve", Zs[:, 0:8], Zs[:, 0:8], Zs[:, 16:24], ALU.add, R=[R_s2], W=[R_s2])
                S.add("dve", lambda e: e.reciprocal(out=Zs[:, 0:8], in_=Zs[:, 0:8]), R=[R_s2], W=[R_s2])
                tt("dve", oT[:, :, 512], Us[:, 0:8], Zs[:, 0:8], ALU.mult, R=[R_s2], W=[rs(o_oT, (8 * CS * 2))])
                BANKS[:] = ALLB
                sample_first[0] = True

            ckpt("S", l, t)
            for g4 in range(4):
                boff, bv = w_get(2048)
                b3 = bv.rearrange("p (k c) -> p k c", c=512)
                for sub in range(4):
                    c = g4 * 4 + sub
                    bank = f_chunk(boff, 4096, lambda kc, b3=b3, sub=sub: b3[:, kc, sub * 128:(sub + 1) * 128], 4,
                                   lambda kc, lo, hi, g4=g4: dT[:, 4 * g4 + kc, lo:hi], [rs(o_dT, (16 * CS * 2))], c)
                    act(ypT[:, c, 0:512], pb(bank), AF.Identity, R=[rp(bank), rs(o_pscale, 64)],
                        W=[rs(o_ypT + c * (CS * 2), (CS * 2))], scale=pscale_t[:, c:c + 1])
            if t0:
                tt("dve", ypT[:, :, 512], pb(SB, 0, 16), pscale_t, ALU.mult, R=[rp(SB, 0, 16), rs(o_pscale, 64)],
                   W=[rs(o_ypT, (16 * CS * 2))])
                sample_first[0] = True

            ckpt("P", l, t)
            PT = [sbv(o_C + 12480 + i * 1024, BF16, 512) for i in range(2)]
            R_PT = [rs(o_C + 12480 + i * 1024, 1024) for i in range(2)]
            ET = [sbv(o_C + 14528 + i * 1024, BF16, 512) for i in range(2)]
            R_ET = [rs(o_C + 14528 + i * 1024, 1024) for i in range(2)]
            rz = sbv(o_C + 20672, F32, 512); R_rz = rs(o_C + 20672, 2048)
            klo = [max(0, p0 - 128), max(0, p0 - 512), 0]
            kn = [p0 + T - klo[g] for g in range(3)]
            koff = [0, 640, 640 + 1024]
            pcount = 0
            for h in range(H):
                hb = h % 2
                KTh = sbv(o_KTh + hb * 7424, BF16, 3712); oK = o_KTh + hb * 7424
                Vh = sbv(o_Vh + hb * 7424, BF16, 29, 128); oV = o_Vh + hb * 7424
                for g in range(3):
                    dma("sp", f"kh{hb}{g}", KTh[:, koff[g]:koff[g] + kn[g]], KTs[l, g, :, h, klo[g]:p0 + T],
                        R=[(f"KTs{l}{g}{h}", klo[g], p0 + T)], W=[rs(oK + koff[g] * 2, kn[g] * 2)])
                if True:
                    b0 = max(0, t * 4 - 1)
                    nb1 = t * 4 + 4 - b0
                    dma("sp", f"vh{hb}0", Vh[:, 5 - nb1:5, :], Vs[l, 0, b0:t * 4 + 4, :, h * 128:(h + 1) * 128].rearrange("b p d -> p b d"),
                        R=[(f"Vs{l}0", b0 * 128, (t * 4 + 4) * 128)], W=[rs(oV + (5 - nb1) * 256, nb1 * 256)])
                    b0 = max(0, t * 4 - 4)
                    nb2 = t * 4 + 4 - b0
                    dma("sp", f"vh{hb}1", Vh[:, 13 - nb2:13, :], Vs[l, 1, b0:t * 4 + 4, :, h * 128:(h + 1) * 128].rearrange("b p d -> p b d"),
                        R=[(f"Vs{l}1", b0 * 128, (t * 4 + 4) * 128)], W=[rs(oV + (13 - nb2) * 256, nb2 * 256)])
                    nb3 = 4 * (t + 1)
                    dma("sp", f"vh{hb}2", Vh[:, 13:13 + nb3, :], Vs[l, 2, 0:nb3, :, h * 128:(h + 1) * 128].rearrange("b p d -> p b d"),
                        R=[(f"Vs{l}2", 0, nb3 * 128)], W=[rs(oV + 13 * 256, nb3 * 256)])
                R_K = rs(oK, 7424); R_V = rs(oV, 7424)
                BANKS[:] = [0, 1, 2]
                bU, bZ = (5, 6) if h % 2 == 0 else (7, 3)
                first = [True]

                def pv(pt, R_pt, vblk, nk, out_cols_fn):
                    items = []
                    for (lhsV, cols_ap_u, cols_ap_z, rhs) in out_cols_fn:
                        st = first[0]
                        first[0] = False
                        items.append((cols_ap_u, lhsV, rhs, st, True))
                        items.append((cols_ap_z, ones_bf[0:nk, :], rhs, st, True))
                    mmgroup(items, R=[R_V, R_pt, R_ones], W=[rp(bU), rp(bZ)])

                def softmax_tile(bS, lo, hi, mask_i, nk=128):
                    s_ = pcount_box[0] % 2
                    pcount_box[0] += 1
                    act(ET[s_][0:nk, lo:hi], psum[0:nk, bS * 512 + lo:bS * 512 + hi], AF.Exp, R=[rp(bS, lo, hi)], W=[R_ET[s_]], scale=SCALE)
                    tt("pool", PT[s_][0:nk, lo:hi], ET[s_][0:nk, lo:hi], masks[0:nk, mask_i, lo:hi], ALU.mult,
                       R=[R_ET[s_], rs(o_masks, 4096)], W=[R_PT[s_]])
                    return PT[s_], R_PT[s_]
                pcount_box = [pcount]
                Qg = [QT[:, g, h, :] for g in range(3)]
                R_Q = rs(o_QT, 24576)
                Ub = psum[:, bU * 512:(bU + 1) * 512]
                Zb = psum[:, bZ * 512:(bZ + 1) * 512]
                kb1 = klo[0]
                for diag in range(2):
                    qbs = [qb for qb in range(4) if not (diag == 1 and t == 0 and qb == 0)]
                    bS = nbank()
                    items = []
                    for qb in qbs:
                        kpos = p0 + qb * 128 - diag * 128 - kb1
                        items.append((pb(bS, qb * 128, qb * 128 + 128), KTh[:, koff[0] + kpos:koff[0] + kpos + 128],
                                      Qg[0][:, qb * 128:(qb + 1) * 128], qb == qbs[0], True))
                    lo = qbs[0] * 128
                    mmgroup(items, R=[R_K, R_Q], W=[rp(bS, lo, 512)])
                    pt, R_pt = softmax_tile(bS, lo, 512, diag)
                    lst = []
                    for qb in qbs:
                        vslot = 1 + qb - diag
                        lst.append((Vh[:, vslot, :], Ub[:, qb * 128:(qb + 1) * 128], Zb[:, qb * 128:(qb + 1) * 128],
                                    pt[:, qb * 128:(qb + 1) * 128]))
                    pv(pt, R_pt, None, 128, lst)
                kb2 = klo[1]
                for diag in range(2):
                    if diag == 1 and t == 0:
                        continue
                    bS = nbank()
                    items = []
                    for r in range(4):
                        kbase = koff[1] + (p0 - diag * 512 - kb2)
                        kap = KTh[:, kbase:kbase + 512].rearrange("p (i r) -> p r i", r=4)[:, r, :]
                        qap = Qg[1].rearrange("p (i r) -> p r i", r=4)[:, r, :]
                        items.append((pb(bS, r * 128, r * 128 + 128), kap, qap, r == 0, True))
                    mmgroup(items, R=[R_K, R_Q], W=[rp(bS)])
                    pt, R_pt = softmax_tile(bS, 0, 512, diag)
                    lst = []
                    for r in range(4):
                        vslot = 9 + r - 4 * diag
                        lst.append((Vh[:, vslot, :], Ub.rearrange("p (i r) -> p r i", r=4)[:, r, :],
                                    Zb.rearrange("p (i r) -> p r i", r=4)[:, r, :], pt[:, r * 128:(r + 1) * 128]))
                    pv(pt, R_pt, None, 128, lst)
                for tp in range(t + 1):
                    bS = nbank()
                    items = []
                    for r in range(4):
                        kbase = koff[2] + 512 * tp
                        kap = KTh[:, kbase:kbase + 512].rearrange("p (i r) -> p r i", r=4)[:, r, :]
                        qap = Qg[2].rearrange("p (i r) -> p r i", r=4)[:, r, :]
                        items.append((pb(bS, r * 128, r * 128 + 128), kap, qap, r == 0, True))
                    mmgroup(items, R=[R_K, R_Q], W=[rp(bS)])
                    pt, R_pt = softmax_tile(bS, 0, 512, 2 if tp == t else 3)
                    lst = []
                    for r in range(4):
                        lst.append((Vh[:, 13 + tp * 4 + r, :], Ub.rearrange("p (i r) -> p r i", r=4)[:, r, :],
                                    Zb.rearrange("p (i r) -> p r i", r=4)[:, r, :], pt[:, r * 128:(r + 1) * 128]))
                    pv(pt, R_pt, None, 128, lst)
                pcount = pcount_box[0]
                S.add("dve", lambda e, Zb=Zb: e.reciprocal(out=rz, in_=Zb), R=[rp(bZ)], W=[R_rz])
                tt("dve", oT[:, h, 0:512], Ub, rz, ALU.mult, R=[rp(bU), R_rz], W=[rs(o_oT + h * (CS * 2), (CS * 2))])
                BANKS[:] = ALLB

            ckpt("ATT", l, t)
            sg = [sbv(o_C + 16576 + i * 2048, F32, 512) for i in range(2)]
            R_sg = [rs(o_C + 16576 + i * 2048, 2048) for i in range(2)]
            tm = sbv(o_C + 6144, F32, 512); R_tm = rs(o_C + 6144, 2048)
            tm2 = sbv(o_C + 8256, F32, 512); R_tm2 = rs(o_C + 8256, 2048)
            for m in range(32):
                o1, v1 = w_get(8192)
                o2, v2 = w_get(3072)
                g3_ = v1.rearrange("p (s k c) -> p s k c", s=2, c=128)
                bpv = v2[:, 0:2048].rearrange("p (k c) -> p k c", c=128)
                bav = v2[:, 2048:3072].rearrange("p (k c) -> p k c", c=128)
                bgp = f_chunk(o1, 16384, lambda kc: g3_[:, 0, kc, :], KC, lambda kc, lo, hi: XT[:, kc, lo:hi], [R_XT], 4 * m)
                bga = f_chunk(o1, 16384, lambda kc: g3_[:, 1, kc, :], KC, lambda kc, lo, hi: XT[:, kc, lo:hi], [R_XT], 4 * m + 1)
                bbp = f_chunk(o2, 6144, lambda kc: bpv[:, kc, :], 16, lambda kc, lo, hi: ypT[:, kc, lo:hi], [rs(o_ypT, (16 * CS * 2))], 4 * m + 2)
                bba = f_chunk(o2, 6144, lambda kc: bav[:, kc, :], 8, lambda kc, lo, hi: oT[:, kc, lo:hi], [rs(o_oT, (8 * CS * 2))], 4 * m + 3)
                act(sg[0], pb(bgp), AF.Sigmoid, R=[rp(bgp)], W=[R_sg[0]])
                act(sg[1], pb(bga), AF.Sigmoid, R=[rp(bga)], W=[R_sg[1]])
                tt("dve", tm, sg[0], pb(bbp), ALU.mult, R=[R_sg[0], rp(bbp)], W=[R_tm])
                tt("dve", tm2, sg[1], pb(bba), ALU.mult, R=[R_sg[1], rp(bba)], W=[R_tm2])
                tt("pool", mgT[:, m, 0:512], tm, tm2, ALU.add, R=[R_tm, R_tm2], W=[rs(o_mg + m * (CS * 2), (CS * 2))])
            if t0:
                sv = s2[:, 0:128]; R_s2 = rs(o_s2, 3200)
                act(sv, pb(SB, 0, 128), AF.Copy, R=[rp(SB, 0, 128)], W=[R_s2])
                sv4 = sv.rearrange("p (m f) -> p f m", f=4)
                sgs = s2[:, 128:192].rearrange("p (f m) -> p f m", f=2)
                act(sgs[:, 0, :], sv4[:, 0, :], AF.Sigmoid, R=[R_s2], W=[R_s2])
                act(sgs[:, 1, :], sv4[:, 1, :], AF.Sigmoid, R=[R_s2], W=[R_s2])
                tt("dve", sgs[:, 0, :], sgs[:, 0, :], sv4[:, 2, :], ALU.mult, R=[R_s2], W=[R_s2])
                tt("dve", sgs[:, 1, :], sgs[:, 1, :], sv4[:, 3, :], ALU.mult, R=[R_s2], W=[R_s2])
                tt("dve", mgT[:, :, 512], sgs[:, 0, :], sgs[:, 1, :], ALU.add, R=[R_s2], W=[rs(o_mg, (KC * CS * 2))])
                sample_first[0] = True

            ckpt("M", l, t)
            def t_proj(act_view, R_act, nkc, kcs_list, res_src, res_name, gi, dest_kind):
                xres = sbv(o_A, F32, 2, 4, 512)
                xpst = sbv(o_A + 16384, F32, 2, 4, 512)
                for cg in range(8):
                    par = cg % 2
                    for tb in range(4):
                        r0 = p0 + tb * 128
                        dma("sp", f"xr{par}{tb}", xres[:, par, tb, :], res_src[r0:r0 + 128, cg * 512:(cg + 1) * 512],
                            R=[(f"{res_name}c{cg}", r0, r0 + 128)] if res_name else [],
                            W=[rs(o_A + (par * 4 + tb) * 2048, 2048)])
                    banks = [nbank() for _ in range(4)]
                    kc = 0
                    for nk in kcs_list:
                        boff, bv = w_get(nk * 512)
                        b3 = bv.rearrange("p (k c) -> p k c", c=512)
                        items = []
                        W_ = [rp(bk) for bk in banks]
                        for kk in range(nk):
                            for tb in range(4):
                                items.append((pb(banks[tb]), act_view[:, kc + kk, tb * 128:(tb + 1) * 128], b3[:, kk, :],
                                              kc + kk == 0, kc + kk == nkc - 1))
                        if t0:
                            for sub in range(4):
                                col = cg * 4 + sub
                                its = []
                                for kk in range(nk):
                                    st = sample_first[0]
                                    sample_first[0] = False
                                    its.append((pb(SB, col, col + 1), b3[:, kk, sub * 128:(sub + 1) * 128],
                                                act_view[:, kc + kk, 512:513], st, kc + kk == nkc - 1))
                                items += its
                                W_.append(rp(SB, col, col + 1))
                        mmgroup(items, R=[rs(boff, nk * 1024), R_act], W=W_)
                        kc += nk
                    for tb in range(4):
                        r0 = p0 + tb * 128
                        Rx = rs(o_A + (par * 4 + tb) * 2048, 2048)
                        Rp = rs(o_A + 16384 + (par * 4 + tb) * 2048, 2048)
                        stt(xpst[:, par, tb, :], xres[:, par, tb, :], ALPHA, pb(banks[tb]), ALU.mult, ALU.add,
                            R=[Rx, rp(banks[tb])], W=[Rp])
                        S.add("dve", lambda e, par=par, tb=tb, cg=cg: e.bn_stats(out=stats[:, tb, cg, :], in_=xpst[:, par, tb, :]),
                              R=[Rp], W=[rs(o_stats + (tb * 8 + cg) * 24, 24)])
                        dma("sp", f"xp{par}{tb}", xpre[r0:r0 + 128, cg * 512:(cg + 1) * 512], xpst[:, par, tb, :],
                            R=[Rp], W=[(f"xprec{cg}", r0, r0 + 128)])
                xs = [sbv(o_B + i * 16384, F32, D) for i in range(2)]
                R_xs = [rs(o_B + i * 16384, 16384) for i in range(2)]
                gB = sbv(o_B + 32768, F32, D); bB = sbv(o_B + 49152, F32, D)
                R_gB = rs(o_B + 32768, 16384); R_bB = rs(o_B + 49152, 16384)
                xTn = sbv(o_B + 65536, BF16, KC, 128); R_xTn = rs(o_B + 65536, 8192)
                dma("sp", "gB", gB, lnrow[l, gi:gi + 1, :].partition_broadcast(128), W=[R_gB])
                dma("sp", "bB", bB, lnrow[l, gi + 1:gi + 2, :].partition_broadcast(128), W=[R_bB])
                sm = sbv(o_small, F32, 4, 8); R_sm = rs(o_small, 128)
                def ln_load(tb_):
                    r0_ = p0 + tb_ * 128
                    i_ = tb_ % 2
                    dma("sp", f"xs{i_}", xs[i_], xpre[r0_:r0_ + 128, :], R=[(f"xprec{cg}", r0_, r0_ + 128) for cg in range(8)], W=[R_xs[i_]])
                ln_load(0)
                ln_load(1)
                for tb in range(4):
                    r0 = p0 + tb * 128
                    i = tb % 2
                    S.add("dve", lambda e, tb=tb: e.bn_aggr(out=sm[:, tb, 0:2], in_=stats[:, tb, :, :].rearrange("p a b -> p (a b)")),
                          R=[rs(o_stats + tb * 192, 192)], W=[R_sm])
                    ts("dve", sm[:, tb, 2:3], sm[:, tb, 1:2], EPS, None, ALU.add, ALU.bypass, R=[R_sm], W=[R_sm])
                    act(sm[:, tb, 3:4], sm[:, tb, 2:3], AF.Sqrt, R=[R_sm], W=[R_sm])
                    S.add("dve", lambda e, tb=tb: e.reciprocal(out=sm[:, tb, 4:5], in_=sm[:, tb, 3:4]), R=[R_sm], W=[R_sm])
                    stt(sm[:, tb, 5:6], sm[:, tb, 0:1], -1.0, sm[:, tb, 4:5], ALU.mult, ALU.mult, R=[R_sm], W=[R_sm])
                    do_T = dest_kind == "mid" or (l == 0 and NL > 1)
                    tbanks = {}

                    def ln_evac(cq, tb=tb, i=i):
                        bt = tbanks[cq]
                        if dest_kind == "mid":
                            outv = XT[:, cq * 4:cq * 4 + 4, tb * 128:(tb + 1) * 128]
                            Wv = [R_XT]
                        else:
                            outv = xTn[:, cq * 4:cq * 4 + 4, :]
                            Wv = [R_xTn]
                        copy_op(ev_eng(), outv, pb(bt).rearrange("p (j t) -> p j t", t=128), R=[rp(bt)], W=Wv)

                    for cq in range(8):
                        c0_, c1_ = cq * 512, (cq + 1) * 512
                        Rpc = rs(o_B + i * 16384 + cq * 2048, 2048)
                        act(xs[i][:, c0_:c1_], xs[i][:, c0_:c1_], AF.Identity, R=[Rpc, R_sm], W=[Rpc],
                            bias=sm[:, tb, 5:6], scale=sm[:, tb, 4:5])
                        tt("dve", xs[i][:, c0_:c1_], xs[i][:, c0_:c1_], gB[:, c0_:c1_], ALU.mult, R=[Rpc, R_gB], W=[Rpc])
                        tt("dve" if cq % 2 == 0 else "pool", xs[i][:, c0_:c1_], xs[i][:, c0_:c1_], bB[:, c0_:c1_], ALU.add,
                           R=[Rpc, R_bB], W=[Rpc])
                        if do_T:
                            bt = nbank()
                            tbanks[cq] = bt

                            def fn(e, bt=bt, cq=cq, i=i):
                                last = None
                                for j in range(4):
                                    k = cq * 4 + j
                                    last = e.transpose(out=pb(bt, j * 128, (j + 1) * 128), in_=xs[i][:, k * 128:(k + 1) * 128], identity=ident)
                                return last
                            S.add("pe", fn, R=[Rpc, R_ident], W=[rp(bt)])
                            if cq >= 2:
                                ln_evac(cq - 2)
                    if do_T:
                        ln_evac(6)
                        ln_evac(7)
                    if dest_kind == "mid":
                        dma("sp", f"xo{i}", xmid[r0:r0 + 128, :], xs[i], R=[R_xs[i]],
                            W=[(f"xmidc{cg}", r0, r0 + 128) for cg in range(8)])
                    elif l == 0 and NL > 1:
                        dma("sp", f"xo{i}", x1row[r0:r0 + 128, :], xs[i], R=[R_xs[i]],
                            W=[(f"x1rowc{cg}", r0, r0 + 128) for cg in range(8)])
                    else:
                        dma("sp", f"xo{i}", y[r0:r0 + 128, :], xs[i], R=[R_xs[i]])
                    if do_T and dest_kind != "mid":
                        dma("sp", "xTn", xT1[:, :, r0:r0 + 128], xTn, R=[R_xTn], W=[("xT1", r0, r0 + 128)])
                    if tb + 2 < 4:
                        ln_load(tb + 2)

            def sample_ln(ycols, gi, out_f32, R_out_w, out_bf, W_bf):
                R_s2 = rs(o_s2, 3200)
                xp = s2[:, 0:32]; sq = s2[:, 32:64]
                stt(xp, xsres, ALPHA, ycols, ALU.mult, ALU.add, R=[rs(o_xsres, 128), rp(SB, 0, 32)], W=[R_s2])
                tt("dve", sq, xp, xp, ALU.mult, R=[R_s2], W=[R_s2])
                xb = sbv(o_samp + 3952, F32, 2); R_xb = rs(o_samp + 3952, 8)
                S.add("dve", lambda e: e.tensor_reduce(out=xb[:, 0:1], in_=xp, axis=AX.X, op=ALU.add), R=[R_s2], W=[R_xb])
                S.add("dve", lambda e: e.tensor_reduce(out=xb[:, 1:2], in_=sq, axis=AX.X, op=ALU.add), R=[R_s2], W=[R_xb])
                onesf = s2[:, 64:192]
                S.add("dve", lambda e: e.memset(onesf, 1.0), W=[R_s2])
                bt = nbank()
                mmgroup([(pb(bt, 0, 2), onesf, xb, True, True)], R=[R_s2, R_xb], W=[rp(bt, 0, 2)])
                st_ = s2[:, 200:210]
                ts("dve", st_[:, 0:2], pb(bt, 0, 2), 1.0 / D, None, ALU.mult, ALU.bypass, R=[rp(bt, 0, 2)], W=[R_s2])
                tt("dve", st_[:, 2:3], st_[:, 0:1], st_[:, 0:1], ALU.mult, R=[R_s2], W=[R_s2])
                tt("dve", st_[:, 3:4], st_[:, 1:2], st_[:, 2:3], ALU.subtract, R=[R_s2], W=[R_s2])
                ts("dve", st_[:, 3:4], st_[:, 3:4], EPS, None, ALU.add, ALU.bypass, R=[R_s2], W=[R_s2])
                act(st_[:, 4:5], st_[:, 3:4], AF.Sqrt, R=[R_s2], W=[R_s2])
                S.add("dve", lambda e: e.reciprocal(out=st_[:, 5:6], in_=st_[:, 4:5]), R=[R_s2], W=[R_s2])
                ts("dve", xp, xp, st_[:, 0:1], st_[:, 5:6], ALU.subtract, ALU.mult, R=[R_s2], W=[R_s2])
                tt("dve", xp, xp, lnfm_t[:, gi, :], ALU.mult, R=[R_s2, rs(o_lnfm, 512)], W=[R_s2])
                tt("dve", out_f32, xp, lnfm_t[:, gi + 1, :], ALU.add, R=[R_s2, rs(o_lnfm, 512)], W=[R_out_w])
                if out_bf is not None:
                    copy_op("dve", out_bf, out_f32, R=[R_out_w], W=W_bf)

            res_src = xrow if l == 0 else x1row
            t_proj(mgT, rs(o_mg, (KC * CS * 2)), KC, [16, 16], res_src, None if l == 0 else "x1row", 0, "mid")
            if t0:
                sample_ln(pb(SB, 0, 32), 0, xsres, rs(o_xsres, 128), XT[:, :, 512], [R_XT])
                sample_first[0] = True

            ckpt("O", l, t)
            ae = sbv(o_C + 6144, F32, 514); R_ae = rs(o_C + 6144, 2056)
            acc = sbv(o_C + 8256, F32, 512); R_acc = rs(o_C + 8256, 2048)
            gl = sbv(o_C + 10368, F32, 512); R_gl = rs(o_C + 10368, 2048)
            for mf in range(FC):
                boff, bv = w_get(SLOT)
                b4 = bv.rearrange("p (s k c) -> p s k c", s=2, c=128)
                ba_ = f_chunk(boff, SLOT * 2, lambda kc: b4[:, 0, kc, :], KC, lambda kc, lo, hi: XT[:, kc, lo:hi], [R_XT], 2 * mf)
                bb_ = f_chunk(boff, SLOT * 2, lambda kc: b4[:, 1, kc, :], KC, lambda kc, lo, hi: XT[:, kc, lo:hi], [R_XT], 2 * mf + 1)
                R_ah = rs(o_ahist + mf * 8, 8)
                copy_op("dve", ae[:, 0:2], ahist[:, mf, :], R=[R_ah], W=[R_ae])
                copy_op("act", ae[:, 2:514], pb(ba_), R=[rp(ba_)], W=[R_ae])
                Rcw = rs(o_convw, 1032); Rcb = rs(o_convb, 344)
                ts("dve", acc, ae[:, 2:514], convw_t[:, 2, mf:mf + 1], convb_t[:, mf:mf + 1], ALU.mult, ALU.add,
                   R=[R_ae, Rcw, Rcb], W=[R_acc])
                stt(acc, ae[:, 1:513], convw_t[:, 1, mf:mf + 1], acc, ALU.mult, ALU.add, R=[R_ae, Rcw, R_acc], W=[R_acc])
                stt(acc, ae[:, 0:512], convw_t[:, 0, mf:mf + 1], acc, ALU.mult, ALU.add, R=[R_ae, Rcw, R_acc], W=[R_acc])
                act(gl, acc, AF.Gelu, R=[R_acc], W=[R_gl])
                tt("dve", hmT[:, mf, 0:512], gl, pb(bb_), ALU.mult, R=[R_gl, rp(bb_)], W=[rs(o_B + mf * (CS * 2), (CS * 2))])
                copy_op("dve", ahist[:, mf, :], ae[:, 512:514], R=[R_ae], W=[R_ah])
            if t0:
                R_s2 = rs(o_s2, 3200)
                ab = s2[:, 0:172]
                act(ab, pb(SB, 0, 172), AF.Copy, R=[rp(SB, 0, 172)], W=[R_s2])
                ab2 = ab.rearrange("p (m f) -> p f m", f=2)
                sc_ = sbv(o_small + 256, F32, 2, FC); R_sc = rs(o_small + 256, 688)
                dma("sp", "c2", sc_, sconv[l], W=[R_sc])
                cacc = s2[:, 180:266]; ctmp = s2[:, 270:356]
                Rcw = rs(o_convw, 1032); Rcb = rs(o_convb, 344)
                tt("dve", cacc, ab2[:, 0, :], convw_t[:, 2, :], ALU.mult, R=[R_s2, Rcw], W=[R_s2])
                tt("dve", cacc, cacc, convb_t, ALU.add, R=[R_s2, Rcb], W=[R_s2])
                tt("dve", ctmp, sc_[:, 1, :], convw_t[:, 1, :], ALU.mult, R=[R_sc, Rcw], W=[R_s2])
                tt("dve", cacc, cacc, ctmp, ALU.add, R=[R_s2], W=[R_s2])
                tt("dve", ctmp, sc_[:, 0, :], convw_t[:, 0, :], ALU.mult, R=[R_sc, Rcw], W=[R_s2])
                tt("dve", cacc, cacc, ctmp, ALU.add, R=[R_s2], W=[R_s2])
                act(ctmp, cacc, AF.Gelu, R=[R_s2], W=[R_s2])
                tt("dve", hmT[:, :, 512], ctmp, ab2[:, 1, :], ALU.mult, R=[R_s2], W=[rs(o_B, (FC * CS * 2))])
                co = s2[:, 360:532].rearrange("p (r c) -> p r c", r=2)
                copy_op("dve", co[:, 0, :], sc_[:, 1, :], R=[R_sc], W=[R_s2])
                copy_op("dve", co[:, 1, :], ab2[:, 0, :], R=[R_s2], W=[R_s2])
                dma("sp", "c0", convs[l], co, R=[R_s2])
                sample_first[0] = True

            ckpt("U", l, t)
            t_proj(hmT, rs(o_B, (FC * CS * 2)), FC, DKC, xmid, "xmid", 2, "out")
            if t0:
                if l == 0 and NL > 1:
                    sample_ln(pb(SB, 0, 32), 2, xsres, rs(o_xsres, 128), None, None)
                else:
                    yo = s2[:, 220:252]
                    sample_ln(pb(SB, 0, 32), 2, yo, rs(o_s2, 3200), None, None)
                    dma("sp", "c1", ysT, yo, R=[rs(o_s2, 3200)])
                sample_first[0] = True

        dma("sp", "c0", poolp[l], uhist, R=[rs(o_uhist, 960)])
        dma("sp", "c1", convp[l], ahist, R=[rs(o_ahist, 688)])

    except StopBuild:
        pass

    eng_cnt = {}
    chan_cnt = {}
    for op in S.ops:
        if op.chan is not None:
            chan_cnt[op.chan] = chan_cnt.get(op.chan, 0) + 16
            op.sig = (op.chan, chan_cnt[op.chan])
        elif op.need:
            eng_cnt[op.eng] = eng_cnt.get(op.eng, 0) + 1
            op.sig = (op.eng, eng_cnt[op.eng])
    print("ops", len(S.ops), "eng sig counts", eng_cnt, "nchan", len(chan_cnt), "max chan", max(chan_cnt.values()), flush=True)

    sems = {}
    for name in list(eng_cnt.keys()) + list(chan_cnt.keys()):
        sems[name] = es.enter_context(nc.semaphore("s_" + name))

    block = es.enter_context(nc.Block())

    def emit(engname):
        def body(e):
            seen = {}
            for op in S.ops:
                if op.eng != engname:
                    continue
                for d in op.deps:
                    sname, val = S.ops[d].sig
                    if seen.get(sname, 0) >= val:
                        continue
                    e.wait_ge(sems[sname], val)
                    seen[sname] = val
                inst = op.fn(e)
                if op.sig is not None:
                    inst.then_inc(sems[op.sig[0]], 16 if op.chan is not None else 1)
                    if op.sig[0] == engname:
                        pass
            if engname == "sp":
                for cname, val in chan_cnt.items():
                    if seen.get(cname, 0) < val:
                        e.wait_ge(sems[cname], val)
        return body

    block.tensor(emit("pe"))
    block.scalar(emit("act"))
    block.vector(emit("dve"))
    block.gpsimd(emit("pool"))
    block.sync(emit("sp"))
    es.close()
    return nc


_NC_CACHE = {}


def prep_inputs(x_prompt, x_sample, state_pool, cache_kv1, cache_kv2, cache_kv3, state_conv,
                w_in, w_pool_grp, pool_scale, w_br_pool, w_br_att, w_out, ln1_g, ln1_b,
                w_up, conv_w, conv_b, w_down, ln2_g, ln2_b, cores=range(8), NL=L):
    f = np.float32
    A = lambda a: np.asarray(a, dtype=f)
    x_prompt = A(x_prompt); x_sample = A(x_sample); state_pool = A(state_pool)
    caches = [A(cache_kv1), A(cache_kv2), A(cache_kv3)]
    state_conv = A(state_conv)
    w_in = A(w_in); w_pool_grp = A(w_pool_grp); pool_scale = A(pool_scale); w_br_pool = A(w_br_pool)
    w_br_att = A(w_br_att); w_out = A(w_out); w_up = A(w_up); conv_w = A(conv_w); conv_b = A(conv_b); w_down = A(w_down)
    ln = [A(ln1_g), A(ln1_b), A(ln2_g), A(ln2_b)]

    ws = np.stack([build_wstream(l, w_in, w_pool_grp, w_br_pool, w_br_att, w_out, w_up, w_down) for l in range(NL)])
    masks, invc, ident = make_consts()
    fm = lambda v, C: np.ascontiguousarray(v.reshape(C, 128).T)
    pscale = np.stack([fm(pool_scale[l], 16) for l in range(L)])
    convw = np.stack([np.stack([fm(conv_w[l, r], FC) for r in range(3)], axis=1) for l in range(L)])
    convb = np.stack([fm(conv_b[l], FC) for l in range(L)])
    lnrow = np.stack([np.stack([ln[i][l] for i in range(4)]) for l in range(L)])
    lnfm = np.stack([np.stack([fm(ln[i][l], KC) for i in range(4)], axis=1) for l in range(L)])

    in_maps = []
    for c in cores:
        b = c // 2
        xb = x_prompt[b]
        xT0 = np.ascontiguousarray(xb.T.reshape(KC, 128, SEQ).transpose(1, 0, 2))
        m = {
            "ws": ws, "xT0": xT0, "xrow": np.ascontiguousarray(xb),
            "xsT": fm(x_sample[c, 0], KC),
            "spool": np.ascontiguousarray(state_pool[:, c].reshape(L, 15, 16, 128).transpose(0, 3, 2, 1)),
            "sconv": np.ascontiguousarray(state_conv[:, c].reshape(L, 2, FC, 128).transpose(0, 3, 1, 2)),
            "pscale": pscale, "convw": convw, "convb": convb, "lnrow": lnrow, "lnfm": lnfm,
            "masks": masks, "invc": invc, "ident": ident,
        }
        for g in range(3):
            m[f"ck{g}"] = np.ascontiguousarray(caches[g][:, c].reshape(L, CLEN[g], 2048))
        in_maps.append(m)
    return in_maps


def kernel(**inputs):
    in_maps = prep_inputs(**inputs)
    if "nc" not in _NC_CACHE:
        _NC_CACHE["nc"] = build_nc()
    nc = _NC_CACHE["nc"]
    res = run_bass_kernel_spmd(nc, in_maps, core_ids=list(range(8))).results

    B = 4
    y_prompt = np.stack([res[2 * b]["y"] for b in range(B)])
    y_sample = np.stack([res[c]["ysT"].T.reshape(1, D) for c in range(8)])
    tofeat = lambda a: a
    pool_p = np.stack([res[2 * b]["poolp"].transpose(0, 3, 2, 1).reshape(L, 15, 2048) for b in range(B)], axis=1)
    pool_s = np.stack([res[c]["pools"].transpose(0, 3, 2, 1).reshape(L, 15, 2048) for c in range(8)], axis=1)
    conv_p = np.stack([res[2 * b]["convp"].transpose(0, 3, 2, 1).reshape(L, 2, DFF) for b in range(B)], axis=1)
    conv_s = np.stack([res[c]["convs"].transpose(0, 2, 3, 1).reshape(L, 2, DFF) for c in range(8)], axis=1)

    def kv_p(b, g):
        r = res[2 * b]
        K = r["kT"][:, g]
        K = K.transpose(0, 3, 2, 1)
        V = r["vo"][:, g]
        if g == 0:
            Vn = V.reshape(L, SEQ, 1024)
        else:
            Vn = V.reshape(L, 4, 4, 128, 1024).transpose(0, 1, 3, 2, 4).reshape(L, SEQ, 1024)
        Vn = Vn.reshape(L, SEQ, H, DH)
        kvf = np.stack([K, Vn], axis=2)
        win = CLEN[g]
        return kvf[:, SEQ - min(win, SEQ):]
    kvp = [np.stack([kv_p(b, g) for b in range(B)], axis=1) for g in range(3)]
    kvs_ = [np.stack([res[c][f"kvs{g}"].reshape(L, CLEN[g], 2, H, DH) for c in range(8)], axis=1) for g in range(3)]
    outs = (y_prompt, y_sample, pool_p, pool_s, kvp[0], kvs_[0], kvp[1], kvs_[1], kvp[2], kvs_[2], conv_p, conv_s)
    return tuple(np.ascontiguousarray(o, dtype=np.float32) for o in outs)


if __name__ == "__main__":
    import time
    t0 = time.time()
    build_nc()
    print("build ok", time.time() - t0)
```
